# Optimizing a Trainium2 kernel written in Bass

```python
import math
import jax, jax.numpy as jnp
from jax import lax
import numpy as np

D_MODEL = 2048
BATCH = 4
SEQ = 4096
DEPTH = 1

N_META = 16
EPS = 1e-6
ROPE_THETA = 500000.0
NEG = -1e30

DA_WIDTH = D_MODEL // 2
DA_QK_DIM = 64
DA_V_DIM = 2 * DA_QK_DIM
DA_HEADS = DA_WIDTH // DA_V_DIM
DA_ROT_DIM = DA_QK_DIM // 4
Q_BLOCK = 128

ML_WIDTH = D_MODEL - DA_WIDTH
ML_HEADS = 4
ML_V_DIM = ML_WIDTH // ML_HEADS
ML_QK_DIM = ML_V_DIM // 2
ML_CHUNK = 64
ML_PAD = ML_CHUNK - N_META
CONV_W = 5
N_GATES = 4

MIX_WIDTH = DA_WIDTH + ML_WIDTH
D_FF = ((8 * D_MODEL // 3 + 255) // 256) * 256

IN_SIZES = [
    DA_HEADS * 2 * DA_QK_DIM,
    DA_HEADS * 2 * DA_QK_DIM,
    DA_HEADS * DA_V_DIM,
    ML_HEADS * ML_QK_DIM,
    ML_HEADS * ML_QK_DIM,
    ML_HEADS * ML_V_DIM,
    ML_WIDTH,
    N_GATES * ML_HEADS,
]
IN_COLS = int(sum(IN_SIZES))
IN_SPLITS = [int(s) for s in np.cumsum(IN_SIZES)[:-1]]

kernel_name = "hymba_diffattn_mlstm_encoder"


def rms_norm(x, g):
    xf = x.astype(jnp.float32)
    y = xf * lax.rsqrt(jnp.mean(xf * xf, axis=-1, keepdims=True) + EPS)
    return (y * g.astype(jnp.float32)).astype(x.dtype)


def head_rms(o, g):
    of = o.astype(jnp.float32)
    y = of * lax.rsqrt(jnp.mean(of * of, axis=-1, keepdims=True) + EPS)
    return (y * g.astype(jnp.float32)).astype(o.dtype)


def rope_tables(length):
    pos = jnp.arange(length, dtype=jnp.float32)
    inv = ROPE_THETA ** (-jnp.arange(0, DA_ROT_DIM, 2, dtype=jnp.float32) / DA_ROT_DIM)
    ang = pos[:, None] * inv[None, :]
    ang = jnp.concatenate([ang, ang], axis=-1)
    return jnp.cos(ang), jnp.sin(ang)


def apply_partial_rope(t, cos, sin):
    c = cos[None, :, None, None, :].astype(t.dtype)
    s = sin[None, :, None, None, :].astype(t.dtype)
    tr, tp = t[..., :DA_ROT_DIM], t[..., DA_ROT_DIM:]
    x1, x2 = tr[..., :DA_ROT_DIM // 2], tr[..., DA_ROT_DIM // 2:]
    rot = jnp.concatenate([-x2, x1], axis=-1)
    return jnp.concatenate([tr * c + rot * s, tp], axis=-1)


def diff_attention(q, k, v, lam, lam_init, head_gain, cos, sin):
    bsz, length = q.shape[0], q.shape[1]
    q = apply_partial_rope(q, cos, sin) * (DA_QK_DIM ** -0.5)
    k = apply_partial_rope(k, cos, sin)
    q = jnp.transpose(q, (0, 2, 3, 1, 4))
    k = jnp.transpose(k, (0, 2, 3, 1, 4))
    v = jnp.transpose(v, (0, 2, 1, 3))

    def attend(qb):
        s = jnp.einsum('bhcqd,bhckd->bhcqk', qb, k).astype(jnp.float32)
        p = jax.nn.softmax(s, axis=-1)
        a = p[:, :, 0] - lam * p[:, :, 1]
        return jnp.einsum('bhqk,bhkd->bhqd', a.astype(v.dtype), v)

    out_meta = attend(q[:, :, :, :N_META])
    qr = q[:, :, :, N_META:]
    s_len = qr.shape[3]
    nb = s_len // Q_BLOCK
    qr = qr.reshape(bsz, DA_HEADS, 2, nb, Q_BLOCK, DA_QK_DIM)
    qr = jnp.moveaxis(qr, 3, 0)
    out_real = lax.map(attend, qr)
    out_real = jnp.moveaxis(out_real, 0, 2).reshape(bsz, DA_HEADS, s_len, DA_V_DIM)
    o = jnp.concatenate([out_meta, out_real], axis=2)
    o = jnp.transpose(o, (0, 2, 1, 3))
    o = head_rms(o, head_gain) * (1.0 - lam_init)
    return o.reshape(bsz, length, DA_WIDTH)


def mlstm_chunkwise(q, k, v, log_i, log_f):
    bsz, nh, t_len, dk = q.shape
    dv = v.shape[-1]
    nc = t_len // ML_CHUNK

    def to_chunks(a):
        a = a.reshape(a.shape[:2] + (nc, ML_CHUNK) + a.shape[3:])
        return jnp.moveaxis(a, 2, 0)

    xs = (to_chunks(q), to_chunks(k), to_chunks(v), to_chunks(log_i), to_chunks(log_f))
    mask = jnp.tril(jnp.ones((ML_CHUNK, ML_CHUNK), dtype=bool))

    def step(carry, inp):
        c_mat, n_vec, m = carry
        qc, kc, vc, li, lf = inp
        b = jnp.cumsum(lf, axis=-1)
        d = b[..., :, None] - b[..., None, :] + li[..., None, :]
        d = jnp.where(mask, d, NEG)
        m_inter = b + m[..., None]
        m_t = jnp.maximum(m_inter, jnp.max(d, axis=-1))
        w_inter = jnp.exp(m_inter - m_t)
        s = jnp.einsum('bhtd,bhsd->bhts', qc, kc) * jnp.exp(d - m_t[..., None])
        num = (w_inter[..., None] * jnp.einsum('bhtd,bhde->bhte', qc, c_mat)
               + jnp.einsum('bhts,bhse->bhte', s, vc))
        den = w_inter * jnp.einsum('bhtd,bhd->bht', qc, n_vec) + jnp.sum(s, axis=-1)
        h = num / jnp.maximum(jnp.abs(den), jnp.exp(-m_t))[..., None]
        b_end = b[..., -1]
        g = b_end[..., None] - b + li
        m_new = jnp.maximum(b_end + m, jnp.max(g, axis=-1))
        decay = jnp.exp(b_end + m - m_new)
        wk = jnp.exp(g - m_new[..., None])
        c_new = decay[..., None, None] * c_mat + jnp.einsum('bhs,bhsd,bhse->bhde', wk, kc, vc)
        n_new = decay[..., None] * n_vec + jnp.einsum('bhs,bhsd->bhd', wk, kc)
        return (c_new, n_new, m_new), h

    init = (jnp.zeros((bsz, nh, dk, dv), jnp.float32),
            jnp.zeros((bsz, nh, dk), jnp.float32),
            jnp.zeros((bsz, nh), jnp.float32))
    _, hs = lax.scan(step, init, xs)
    return jnp.moveaxis(hs, 0, 2).reshape(bsz, nh, t_len, dv)


def depthwise_conv_centred(x, w, b):
    ch = x.shape[-1]
    y = lax.conv_general_dilated(
        x, w.astype(x.dtype)[:, None, :], window_strides=(1,),
        padding=[(CONV_W // 2, CONV_W // 2)],
        dimension_numbers=('NWC', 'WIO', 'NWC'), feature_group_count=ch)
    return y + b.astype(x.dtype)


def mlstm_mixer(q, k, v, o_pre, gates, conv_w, conv_b, gate_bias, head_gain):
    bsz, length = q.shape[0], q.shape[1]
    out_dtype = v.dtype
    qk = jax.nn.silu(depthwise_conv_centred(jnp.concatenate([q, k], axis=-1), conv_w, conv_b))
    q, k = qk[..., :ML_HEADS * ML_QK_DIM], qk[..., ML_HEADS * ML_QK_DIM:]

    def heads(a, dh):
        a = a.astype(jnp.float32).reshape(bsz, length, ML_HEADS, dh)
        a = jnp.transpose(a, (0, 2, 1, 3))
        return jnp.pad(a, ((0, 0), (0, 0), (ML_PAD, 0), (0, 0)))

    qh = heads(q, ML_QK_DIM) * (ML_QK_DIM ** -0.5)
    kh = heads(k, ML_QK_DIM)
    vh = heads(v, ML_V_DIM)
    g = gates.astype(jnp.float32).reshape(bsz, length, N_GATES, ML_HEADS) + gate_bias.astype(jnp.float32)
    g = jnp.transpose(g, (2, 0, 3, 1))
    pad_t = ((0, 0), (0, 0), (ML_PAD, 0))
    li_f = jnp.pad(g[0], pad_t, constant_values=NEG)
    lf_f = jnp.pad(jax.nn.log_sigmoid(g[1]), pad_t)
    li_b = jnp.pad(g[2], pad_t, constant_values=NEG)
    lf_b = jnp.pad(jax.nn.log_sigmoid(g[3]), pad_t)

    flip = lambda a: jnp.flip(a, axis=2)
    h_fwd = mlstm_chunkwise(qh, kh, vh, li_f, lf_f)
    h_bwd = flip(mlstm_chunkwise(flip(qh), flip(kh), flip(vh), flip(li_b), flip(lf_b)))
    h = (h_fwd + h_bwd)[:, :, ML_PAD:]
    h = jnp.transpose(h, (0, 2, 1, 3))
    h = head_rms(h, head_gain).reshape(bsz, length, ML_WIDTH)
    return (jax.nn.sigmoid(o_pre.astype(jnp.float32)) * h).astype(out_dtype)


def setup_inputs(seed: int = 0) -> dict:
    key = jax.random.key(seed)
    ks = jax.random.split(key, 20)
    nrm = jax.random.normal
    f32 = jnp.float32
    gate_base = jnp.stack([jnp.zeros((ML_HEADS,), f32), jnp.linspace(3.0, 6.0, ML_HEADS, dtype=f32),
                           jnp.zeros((ML_HEADS,), f32), jnp.linspace(3.0, 6.0, ML_HEADS, dtype=f32)])
    return {
        "x": nrm(ks[0], (BATCH, SEQ, D_MODEL), f32),
        "meta_tokens": nrm(ks[1], (N_META, D_MODEL), f32),
        "norm_mix": 1.0 + 0.02 * nrm(ks[2], (DEPTH, D_MODEL), f32),
        "w_in": nrm(ks[3], (DEPTH, D_MODEL, IN_COLS), f32) * D_MODEL ** -0.5,
        "da_lambda_q1": 0.1 * nrm(ks[4], (DEPTH, DA_QK_DIM), f32),
        "da_lambda_k1": 0.1 * nrm(ks[5], (DEPTH, DA_QK_DIM), f32),
        "da_lambda_q2": 0.1 * nrm(ks[6], (DEPTH, DA_QK_DIM), f32),
        "da_lambda_k2": 0.1 * nrm(ks[7], (DEPTH, DA_QK_DIM), f32),
        "da_head_norm": 1.0 + 0.02 * nrm(ks[8], (DEPTH, DA_HEADS, DA_V_DIM), f32),
        "ml_conv_w": nrm(ks[9], (DEPTH, CONV_W, 2 * ML_HEADS * ML_QK_DIM), f32) * CONV_W ** -0.5,
        "ml_conv_b": 0.01 * nrm(ks[10], (DEPTH, 2 * ML_HEADS * ML_QK_DIM), f32),
        "ml_gate_bias": gate_base[None] + 0.1 * nrm(ks[11], (DEPTH, N_GATES, ML_HEADS), f32),
        "ml_head_norm": 1.0 + 0.02 * nrm(ks[12], (DEPTH, ML_HEADS, ML_V_DIM), f32),
        "w_out": nrm(ks[13], (DEPTH, MIX_WIDTH, D_MODEL), f32) * MIX_WIDTH ** -0.5,
        "norm_ffn": 1.0 + 0.02 * nrm(ks[14], (DEPTH, D_MODEL), f32),
        "w_gate": nrm(ks[15], (DEPTH, D_MODEL, D_FF), f32) * D_MODEL ** -0.5,
        "w_up": nrm(ks[16], (DEPTH, D_MODEL, D_FF), f32) * D_MODEL ** -0.5,
        "w_down": nrm(ks[17], (DEPTH, D_FF, D_MODEL), f32) * D_FF ** -0.5,
        "norm_final": 1.0 + 0.02 * nrm(ks[18], (D_MODEL,), f32),
    }


def reference(x, meta_tokens, norm_mix, w_in, da_lambda_q1, da_lambda_k1, da_lambda_q2,
              da_lambda_k2, da_head_norm, ml_conv_w, ml_conv_b, ml_gate_bias, ml_head_norm,
              w_out, norm_ffn, w_gate, w_up, w_down, norm_final):
    bsz = x.shape[0]
    meta = jnp.broadcast_to(meta_tokens.astype(x.dtype)[None], (bsz, N_META, D_MODEL))
    h = jnp.concatenate([meta, x], axis=1)
    length = h.shape[1]
    cos, sin = rope_tables(length)

    for l in range(DEPTH):
        u = rms_norm(h, norm_mix[l])
        proj = jnp.einsum('bld,dc->blc', u, w_in[l])
        aq, ak, av, mq, mk, mv, mo, mg = jnp.split(proj, IN_SPLITS, axis=-1)
        lam_init = 0.8 - 0.6 * math.exp(-0.3 * l)
        lam = (jnp.exp(jnp.sum(da_lambda_q1[l].astype(jnp.float32) * da_lambda_k1[l].astype(jnp.float32)))
               - jnp.exp(jnp.sum(da_lambda_q2[l].astype(jnp.float32) * da_lambda_k2[l].astype(jnp.float32)))
               + lam_init)
        attn_out = diff_attention(
            aq.reshape(bsz, length, DA_HEADS, 2, DA_QK_DIM),
            ak.reshape(bsz, length, DA_HEADS, 2, DA_QK_DIM),
            av.reshape(bsz, length, DA_HEADS, DA_V_DIM),
            lam, lam_init, da_head_norm[l], cos, sin)
        ml_out = mlstm_mixer(mq, mk, mv, mo, mg, ml_conv_w[l], ml_conv_b[l],
                             ml_gate_bias[l], ml_head_norm[l])
        mixed = jnp.concatenate([attn_out, ml_out.astype(attn_out.dtype)], axis=-1)
        h = h + jnp.einsum('blc,cd->bld', mixed, w_out[l])
        u = rms_norm(h, norm_ffn[l])
        ff = jax.nn.silu(jnp.einsum('bld,df->blf', u, w_gate[l])) * jnp.einsum('bld,df->blf', u, w_up[l])
        h = h + jnp.einsum('blf,fd->bld', ff, w_down[l])

    h = rms_norm(h, norm_final)
    return h[:, N_META:]
```

```python
import numpy as np
from contextlib import ExitStack
import concourse.bass as bass
import concourse.mybir as mybir
from concourse.bass_utils import run_bass_kernel_spmd

F32 = mybir.dt.float32
BF16 = mybir.dt.bfloat16
AF = mybir.ActivationFunctionType
ALU = mybir.AluOpType
AX = mybir.AxisListType

D = 2048
NT = 4128
NOWN = 2048
NTILE = 33
IN_COLS = 6160
DFF = 5632
EPS = 1e-6
SEQW = 4132
C_AQ, C_AK, C_AV, C_MQ, C_MK, C_MV, C_MO, C_MG = 0, 1024, 2048, 3072, 3584, 4096, 5120, 6144
NEGB = -30000.0

DEBUG_STOP = None

ENGS = ['sp', 'act', 'pe', 'dve', 'pool']


class _Op:
    __slots__ = ('eng', 'fn', 'dsem', 'ndma', 'deps', 'needs_inc', 'sem', 'val')


class Ctx:
    def __init__(self, nc, es, n_dma=44):
        self.nc = nc
        self.h = {}
        for e in ENGS:
            self.h[('e', e)] = es.enter_context(nc.semaphore("se_" + e))
        for i in range(n_dma):
            self.h[('d', i)] = es.enter_context(nc.semaphore("sd_%d" % i))
        for i in range(8):
            self.h[('g', i)] = es.enter_context(nc.semaphore("sg_%d" % i))
        self.n_dma = n_dma
        self.cnt = {}
        self.waited = {e: {} for e in ENGS}


class Phase:
    def __init__(self, ctx, name):
        self.c = ctx
        self.name = name
        self.ops = {e: [] for e in ENGS}
        self.lw = {}
        self.rd = {}
        self.dmap = {}
        self.downer = {}

    def _dsem(self, key, eng):
        if key not in self.dmap:
            kind = 'g' if eng == 'pool' else 'd'
            n = sum(1 for v in self.dmap.values() if v[0] == kind)
            assert n < (8 if kind == 'g' else self.c.n_dma), "out of DMA semaphores in phase " + self.name
            self.dmap[key] = (kind, n)
            self.downer[key] = eng
        assert self.downer[key] == eng
        return self.dmap[key]

    def add(self, eng, fn, reads=(), writes=(), dsem=None, ndma=1):
        op = _Op()
        op.eng = eng
        op.fn = fn
        op.ndma = ndma
        op.dsem = self._dsem(dsem, eng) if dsem is not None else None
        op.needs_inc = False
        op.sem = None
        op.val = None
        deps = []
        wset = set(writes)
        for k in reads:
            if k in wset:
                continue
            w = self.lw.get(k)
            if w is not None:
                deps.append(w)
        for k in writes:
            w = self.lw.get(k)
            if w is not None:
                deps.append(w)
            r = self.rd.get(k)
            if r:
                deps.extend(r[0].values())
                deps.extend(r[1])
        pe_compute = (eng == 'pe' and op.dsem is None)
        dd = []
        seen = set()
        for d in deps:
            if d is op or id(d) in seen:
                continue
            seen.add(id(d))
            if pe_compute and d.eng == 'pe' and d.dsem is None:
                continue
            d.needs_inc = True
            dd.append(d)
        op.deps = dd
        for k in writes:
            self.lw[k] = op
            self.rd[k] = ({}, [])
        for k in reads:
            if k in wset:
                continue
            r = self.rd.setdefault(k, ({}, []))
            if op.dsem is None:
                r[0][eng] = op
            else:
                r[1].append(op)
        self.ops[eng].append(op)
        return op

    def run(self):
        c = self.c
        nc = c.nc
        for e in ENGS:
            for op in self.ops[e]:
                if op.dsem is not None:
                    c.cnt[op.dsem] = c.cnt.get(op.dsem, 0) + 16 * op.ndma
                    op.sem = op.dsem
                    op.val = c.cnt[op.dsem]
                elif op.needs_inc:
                    k = ('e', e)
                    c.cnt[k] = c.cnt.get(k, 0) + 1
                    op.sem = k
                    op.val = c.cnt[k]

        def emit(e, eng):
            waited = c.waited[e]
            used = {}
            for op in self.ops[e]:
                need = {}
                for d in op.deps:
                    if need.get(d.sem, 0) < d.val:
                        need[d.sem] = d.val
                for s, v in need.items():
                    if waited.get(s, 0) < v:
                        eng.wait_ge(c.h[s], v)
                        waited[s] = v
                if op.dsem is not None:
                    op.fn(eng, c.h[op.dsem])
                    used[op.dsem] = op.val
                else:
                    ins = op.fn(eng)
                    if op.needs_inc:
                        ins.then_inc(c.h[op.sem], 1)
            for s, v in used.items():
                if waited.get(s, 0) < v:
                    eng.wait_ge(c.h[s], v)
                    waited[s] = v

        with nc.Block() as block:
            block.sync(lambda eng: emit('sp', eng))
            block.scalar(lambda eng: emit('act', eng))
            block.tensor(lambda eng: emit('pe', eng))
            block.vector(lambda eng: emit('dve', eng))
            block.gpsimd(lambda eng: emit('pool', eng))


def dma(out, in_):
    return lambda e, s: e.dma_start(out=out, in_=in_).then_inc(s, 16)


def _local_index(hf):
    idx = np.full(NT, -1, np.int64)
    if hf == 0:
        idx[0:2048] = 16 + np.arange(2048)
        idx[2048:4096] = 16 + 2048 + np.arange(2048)
        idx[4096:4112] = np.arange(16)
    else:
        idx[0:2048] = 16 + 4095 - np.arange(2048)
        idx[2048:4096] = 16 + 2047 - np.arange(2048)
        idx[4112:4128] = 15 - np.arange(16)
    return idx


def _consts():
    c = np.zeros((128, 8, 128), np.float32)
    i = np.arange(128)
    c[:, 0] = np.eye(128)
    for blk in (0, 64):
        for d in range(8):
            c[blk + d + 8, 1, blk + d] = -1.0
            c[blk + d, 1, blk + d + 8] = 1.0
    c[:, 2] = (i[:, None] <= i[None, :])
    c[:, 3] = (i[:, None] >= i[None, :])
    same = (i[:, None] // 16) == (i[None, :] // 16)
    c[:, 4] = c[:, 2] * same
    c[:, 5] = c[:, 3] * same
    c[:, 6] = 1.0
    return c.reshape(128, 8 * 128)


def _rope_tables(idx):
    pos = np.where(idx >= 0, idx, 0).astype(np.float32)
    inv = (np.float32(500000.0) ** (-np.arange(0, 16, 2, dtype=np.float32) / np.float32(16))).astype(np.float32)
    ang = (pos[:, None] * inv[None, :]).astype(np.float32)
    cosT = np.ones((128, NT), np.float32)
    sinT = np.zeros((128, NT), np.float32)
    for p in range(128):
        d = p % 64
        if d < 16:
            cosT[p] = np.cos(ang[:, d % 8])
            sinT[p] = np.sin(ang[:, d % 8])
    return cosT, sinT


def build(stop=None):
    nc = bass.Bass("TRN2", target_bir_lowering=False)
    dbg = stop is not None

    def din(name, shape, dt=F32):
        return nc.dram_tensor(name, list(shape), dt, kind="ExternalInput").ap()

    def dscr(name, shape, dt):
        return nc.dram_tensor(name, list(shape), dt, kind=("ExternalOutput" if dbg else "Internal")).ap()

    T = {}
    T['xs'] = din("xs", [NT, D])
    T['w_in'] = din("w_in", [D, IN_COLS])
    T['norm_mix'] = din("norm_mix", [1, D])
    T['consts'] = din("consts", [128, 1024])
    T['cosT'] = din("cosT", [128, NT])
    T['sinT'] = din("sinT", [128, NT])
    out = nc.dram_tensor("out", [NOWN, D], F32, kind="ExternalOutput").ap()
    T['out'] = out
    T['qT'] = dscr("qT_s", [8, 128, NOWN], BF16)
    T['kT'] = dscr("kT_s", [8, 128, NT], BF16)
    T['V'] = dscr("V_s", [NT, 1024], BF16)
    T['MV'] = dscr("MV_s", [NT, 1024], BF16)
    T['MO'] = dscr("MO_s", [NOWN, 1024], F32)
    T['MG'] = dscr("MG_s", [NT, 16], F32)
    T['MQK'] = dscr("MQK_s", [1024, SEQW], BF16)

    for nm in ('lq1', 'lk1', 'lq2', 'lk2'):
        T[nm] = din(nm, [1, 64])
    T['dahn_T'] = din("dahn_T", [128, 8])
    T['kbias'] = din("kbias", [32, 1])
    T['mixT'] = dscr("mixT_s", [128, 16, NOWN], BF16)
    T['HB'] = dscr("HB_s", [NOWN, 1024], F32)
    T['HA'] = dscr("HA_s", [NOWN, 1024], F32)
    T['H1'] = dscr("H1_s", [NOWN, D], F32)
    T['H2'] = dscr("H2_s", [NOWN, D], F32)
    T['w_out'] = din("w_out", [D, D])
    T['w_gate'] = din("w_gate", [D, DFF])
    T['w_up'] = din("w_up", [D, DFF])
    T['w_down'] = din("w_down", [DFF, D])
    T['norm_ffn'] = din("norm_ffn", [1, D])
    T['norm_final'] = din("norm_final", [1, D])
    T['gbias'] = din("gbias", [1, 16])
    T['mbias'] = din("mbias", [32, 2])
    T['convw_T'] = din("convw_T", [128, 40])
    T['convb_T'] = din("convb_T", [128, 8])
    T['mlhn'] = din("mlhn", [1, 1024])

    with ExitStack() as es:
        ctx = Ctx(nc, es)
        phase_A(nc, ctx, T)
        if stop == 'A':
            return nc
        phase_ATT(nc, ctx, T)
        if stop == 'ATT':
            return nc
        phase_ML(nc, ctx, T)
        if stop == 'ML':
            return nc
        phase_OUT(nc, ctx, T)
        if stop == 'OUT':
            return nc
        phase_FFN(nc, ctx, T)
        if stop == 'FFN':
            return nc
        phase_FIN(nc, ctx, T)
    return nc


def phase_A(nc, ctx, T):
    with ExitStack() as es:
        def sb(name, shape, dt):
            return es.enter_context(nc.sbuf_tensor(name, list(shape), dt))

        def ps(name, shape, dt):
            return es.enter_context(nc.psum_tensor(name, list(shape), dt))

        uT = sb("uT", [128, 16, NT], BF16)
        wb0 = sb("wb0", [128, 16, 512], BF16)
        cst = sb("cstA", [128, 256], BF16)
        ident = cst[:, 0:128]
        RT = cst[:, 128:256]

        with ExitStack() as es1:
            def sb1(name, shape, dt):
                return es1.enter_context(nc.sbuf_tensor(name, list(shape), dt))
            xb = [sb1("xb%d" % i, [128, D], F32) for i in range(3)]
            xn = [sb1("xn%d" % i, [128, D], BF16) for i in range(3)]
            junk = sb1("junkA", [128, D], BF16)
            gbc = sb1("gbc", [128, D], F32)
            st = [sb1("stA%d" % i, [128, 4], F32) for i in range(3)]
            pT = [es1.enter_context(nc.psum_tensor("pT%d" % i, [128, 1024], BF16)) for i in range(6)]
            P = Phase(ctx, "A1")
            P.add('sp', dma(gbc[:, :], T['norm_mix'][0:1, :].partition_broadcast(128)), writes=['gbc'], dsem='gbc')
            P.add('pool', dma(cst[:, :], T['consts'][:, 0:256]), writes=['cst'], dsem='cst')
            P.add('pool', dma(wb0[:, :, :], T['w_in'].rearrange("(c p) n -> p c n", p=128)[:, :, C_AQ:C_AQ + 512]),
                  writes=['wb0'], dsem='wb0')

            def stA(tt):
                rows = 128 if tt < 32 else 32
                s = tt % 3
                r0 = tt * 128
                P.add('sp', dma(xb[s][:rows, :], T['xs'][r0:r0 + rows, :]), writes=[('xb', s)], dsem=('xb', s))

            def stB(tt):
                rows = 128 if tt < 32 else 32
                s = tt % 3
                P.add('act', lambda e: e.activation(
                    out=junk[:rows, :], in_=xb[s][:rows, :], func=AF.Square, accum_out=st[s][:rows, 0:1]),
                    reads=[('xb', s)], writes=['junk', ('st0', s)])
                P.add('act', lambda e: e.activation(
                    out=st[s][:rows, 1:2], in_=st[s][:rows, 0:1], func=AF.Sqrt, scale=1.0 / D, bias=EPS),
                    reads=[('st0', s)], writes=[('st1', s)])
                P.add('dve', lambda e: e.reciprocal(out=st[s][:rows, 2:3], in_=st[s][:rows, 1:2]),
                      reads=[('st1', s)], writes=[('st2', s)])
                P.add('dve', lambda e: e.scalar_tensor_tensor(
                    out=xn[s][:rows, :], in0=xb[s][:rows, :], scalar=st[s][:rows, 2:3], in1=gbc[:rows, :],
                    op0=ALU.mult, op1=ALU.mult),
                    reads=[('xb', s), ('st2', s), 'gbc'], writes=[('xn', s)])

            def stC(tt):
                rows = 128 if tt < 32 else 32
                s = tt % 3
                r0 = tt * 128
                for half in range(2):
                    pb = (tt % 3) * 2 + half

                    def tr(e, half=half, pb=pb):
                        ins = None
                        for j in range(8):
                            c = half * 8 + j
                            ins = e.transpose(out=pT[pb][:, j * 128:j * 128 + rows],
                                              in_=xn[s][:rows, c * 128:(c + 1) * 128], identity=ident[:rows, :rows])
                        return ins
                    P.add('pe', tr, reads=[('xn', s), 'cst'], writes=[('pT', pb)])
                    eng = 'act' if half == 0 else 'dve'

                    def ev(e, half=half, pb=pb, eng=eng):
                        src_ = pT[pb][:, :].rearrange("p (c t) -> p c t", c=8)[:, :, 0:rows]
                        dst = uT[:, half * 8:half * 8 + 8, r0:r0 + rows]
                        if eng == 'act':
                            return e.copy(out=dst, in_=src_)
                        return e.tensor_copy(out=dst, in_=src_)
                    P.add(eng, ev, reads=[('pT', pb)], writes=[('uT', tt, half)])

            stA(0)
            stA(1)
            stB(0)
            for tt in range(NTILE):
                if tt + 2 < NTILE:
                    stA(tt + 2)
                if tt + 1 < NTILE:
                    stB(tt + 1)
                stC(tt)
            P.run()

        with ExitStack() as es2:
            def sb2(name, shape, dt):
                return es2.enter_context(nc.sbuf_tensor(name, list(shape), dt))
            wb = [wb0, sb2("wb1", [128, 16, 512], BF16)]
            NST = 3
            sto = [sb2("sto%d" % i, [128, 512], F32) for i in range(NST)]
            stb = [sb2("stb%d" % i, [128, 512], BF16) for i in range(NST)]
            xbb = [sb2("xbb%d" % i, [128, 512], BF16) for i in range(2)]
            t1 = [sb2("t1_%d" % i, [128, 512], F32) for i in range(2)]
            t2 = [sb2("t2_%d" % i, [128, 512], F32) for i in range(2)]
            cs = [sb2("cs%d" % i, [128, 2, 512], F32) for i in range(3)]
            pm = [es2.enter_context(nc.psum_tensor("pm%d" % i, [128, 512], F32)) for i in range(4)]
            pr = [es2.enter_context(nc.psum_tensor("pr%d" % i, [128, 512], F32)) for i in range(2)]
            P = Phase(ctx, "A2")
            w3 = T['w_in'].rearrange("(c p) n -> p c n", p=128)
            tb_all = [(i * 512, 512) for i in range(8)] + [(4096, 32)]
            tb_own = [(i * 512, 512) for i in range(4)]
            cnt = {'w': 0, 'pm': 0, 'st': 0, 'rp': 0, 'cs': -1, 'csl': -1}
            pend = []

            def seqcol(n0):
                if n0 < 4096:
                    return 18 + n0
                return None

            def load_w(c0, ncols):
                s = cnt['w'] % 2
                cnt['w'] += 1
                if cnt['w'] > 1:
                    P.add('pool', dma(wb[s][:, :, 0:ncols], w3[:, :, c0:c0 + ncols]), writes=[('wb', s)], dsem=('wb', s))
                return s

            def fm_block(ws, cg, n0, nw, kind, ch0):
                pb = cnt['pm'] % 4
                cnt['pm'] += 1

                def mm(e):
                    ins = None
                    for c in range(16):
                        ins = e.matmul(pm[pb][:, 0:nw], lhsT=wb[ws][:, c, cg * 128:(cg + 1) * 128],
                                       rhs=uT[:, c, n0:n0 + nw], start=(c == 0), stop=(c == 15))
                    return ins
                tiles = sorted(set([n0 // 128, (n0 + nw - 1) // 128]))
                rk = [('uT', t, h) for t in range(n0 // 128, (n0 + nw - 1) // 128 + 1) for h in range(2)]
                P.add('pe', mm, reads=[('wb', ws)] + rk, writes=[('pm', pb)])
                if kind == 'm':
                    s = cnt['st'] % NST
                    cnt['st'] += 1
                    P.add('act', lambda e: e.copy(out=stb[s][:, 0:nw], in_=pm[pb][:, 0:nw]),
                          reads=[('pm', pb)], writes=[('stb', s)])
                    if n0 < 4096:
                        P.add('sp', dma(T['MQK'][ch0:ch0 + 128, 18 + n0:18 + n0 + nw], stb[s][:, 0:nw]),
                              reads=[('stb', s)], dsem=('stb_o', s))
                    else:
                        P.add('sp', lambda e, sm: (
                            e.dma_start(out=T['MQK'][ch0:ch0 + 128, 2:18], in_=stb[s][:, 0:16]).then_inc(sm, 16),
                            e.dma_start(out=T['MQK'][ch0:ch0 + 128, 4114:4130], in_=stb[s][:, 16:32]).then_inc(sm, 16)),
                            reads=[('stb', s)], dsem=('stb_o', s), ndma=2)
                    return
                r = cnt['rp'] % 2
                cnt['rp'] += 1
                csl = cnt['cs'] % 3
                cst_ = cs[csl]
                P.add('act', lambda e: e.copy(out=xbb[r][:, 0:nw], in_=pm[pb][:, 0:nw]),
                      reads=[('pm', pb)], writes=[('xbb', r)])
                P.add('dve', lambda e: e.tensor_tensor(out=t1[r][:, 0:nw], in0=pm[pb][:, 0:nw], in1=cst_[:, 0, 0:nw], op=ALU.mult),
                      reads=[('pm', pb), ('cs', csl), ('xbb', r)], writes=[('t1', r)])

                def stage2():
                    P.add('pe', lambda e: e.matmul(pr[r][:, 0:nw], lhsT=RT, rhs=xbb[r][:, 0:nw], start=True, stop=True),
                          reads=[('xbb', r), 'cst'], writes=[('pr', r)])
                    P.add('dve', lambda e: e.tensor_tensor(out=t2[r][:, 0:nw], in0=pr[r][:, 0:nw], in1=cst_[:, 1, 0:nw], op=ALU.mult),
                          reads=[('pr', r), ('cs', csl)], writes=[('t2', r)])
                    s = cnt['st'] % NST
                    cnt['st'] += 1
                    P.add('dve', lambda e: e.tensor_tensor(out=stb[s][:, 0:nw], in0=t1[r][:, 0:nw], in1=t2[r][:, 0:nw], op=ALU.add),
                          reads=[('t1', r), ('t2', r)], writes=[('stb', s)])
                    dst = T['qT'] if kind == 'q' else T['kT']
                    P.add('sp', dma(dst[ch0, :, n0:n0 + nw], stb[s][:, 0:nw]), reads=[('stb', s)], dsem=('stb_o', s))
                prev = pend.pop() if pend else None
                pend.append(stage2)
                if prev is not None:
                    prev()

            def tm_block(ws, cb0, ncols, tt, dst, dcol0, f32out):
                rows = 128 if tt < 32 else 32
                r0 = tt * 128
                pb = cnt['pm'] % 4
                cnt['pm'] += 1

                def mm(e):
                    ins = None
                    for c in range(16):
                        ins = e.matmul(pm[pb][:rows, 0:ncols], lhsT=uT[:, c, r0:r0 + rows],
                                       rhs=wb[ws][:, c, cb0:cb0 + ncols], start=(c == 0), stop=(c == 15))
                    return ins
                P.add('pe', mm, reads=[('wb', ws), ('uT', tt, 0), ('uT', tt, 1)], writes=[('pm', pb)])
                s = cnt['st'] % NST
                cnt['st'] += 1
                stg = sto if f32out else stb
                key = 'sto' if f32out else 'stb'
                eng = 'act' if (cnt['st'] % 2 == 0) else 'dve'
                if eng == 'act':
                    P.add('act', lambda e: e.copy(out=stg[s][:rows, 0:ncols], in_=pm[pb][:rows, 0:ncols]),
                          reads=[('pm', pb)], writes=[(key, s)])
                else:
                    P.add('dve', lambda e: e.tensor_copy(out=stg[s][:rows, 0:ncols], in_=pm[pb][:rows, 0:ncols]),
                          reads=[('pm', pb)], writes=[(key, s)])
                P.add('sp', dma(dst[r0:r0 + rows, dcol0:dcol0 + ncols], stg[s][:rows, 0:ncols]),
                      reads=[(key, s)], dsem=(key + '_o', s))

            for (c0, kind, tbs) in ((C_AQ, 'q', tb_own), (C_AK, 'k', tb_all)):
                for wblk in range(2):
                    ws = load_w(c0 + wblk * 512, 512)
                    def load_cs(n0, nw):
                        cnt['csl'] += 1
                        csl = cnt['csl'] % 3
                        P.add('sp', lambda e, sm: (
                            e.dma_start(out=cs[csl][:, 0, 0:nw], in_=T['cosT'][:, n0:n0 + nw]).then_inc(sm, 16),
                            e.dma_start(out=cs[csl][:, 1, 0:nw], in_=T['sinT'][:, n0:n0 + nw]).then_inc(sm, 16)),
                            writes=[('cs', csl)], dsem=('cs', csl), ndma=2)
                    load_cs(*tbs[0])
                    for ti, (n0, nw) in enumerate(tbs):
                        if ti + 1 < len(tbs):
                            load_cs(*tbs[ti + 1])
                        cnt['cs'] += 1
                        for cg in range(4):
                            fm_block(ws, cg, n0, nw, kind, wblk * 4 + cg)
            while pend:
                pend.pop()()
            tb_mq = tb_own + [(2048, 32), (4096, 32)]
            for (c0, chb, tbs) in ((C_MQ, 0, tb_mq), (C_MK, 512, tb_all)):
                ws = load_w(c0, 512)
                for cg in range(4):
                    for (n0, nw) in tbs:
                        fm_block(ws, cg, n0, nw, 'm', chb + cg * 128)
            for (c0, dst, f32out, ntiles) in ((C_AV, T['V'], False, NTILE), (C_MV, T['MV'], False, NTILE),
                                              (C_MO, T['MO'], True, 16)):
                for wblk in range(2):
                    ws = load_w(c0 + wblk * 512, 512)
                    for tt in range(ntiles):
                        tm_block(ws, 0, 512, tt, dst, wblk * 512, f32out)
            ws = load_w(C_MG, 16)
            for tt in range(NTILE):
                tm_block(ws, 0, 16, tt, T['MG'], 0, True)
            P.run()


def phase_ATT(nc, ctx, T, NH=8, NQB=4, nkt=NTILE):
    with ExitStack() as es:
        def sb(name, shape, dt):
            return es.enter_context(nc.sbuf_tensor(name, list(shape), dt))
        mixedT = sb("mixedTa", [128, 8, NOWN], BF16)
        kT = [sb("kTa%d" % i, [128, NT], BF16) for i in range(2)]
        vt = [sb("vta%d" % i, [128, 33, 128], BF16) for i in range(2)]
        qT = [sb("qTa%d" % i, [128, NOWN], BF16) for i in range(2)]
        NPT = 3
        pt = [sb("pta%d" % i, [128, 1024], BF16) for i in range(NPT)]
        zacc = [[sb("zacc%d_%d" % (b, c), [128, 512], F32) for c in range(2)] for b in range(2)]
        ones32 = sb("ones32", [128, 128], F32)
        onesb = sb("onesb", [128, 128], BF16)
        zs0 = sb("zs0a", [128, 512], F32)
        kbias = sb("kbias_sb", [32, 1], F32)
        lv = sb("lamv", [128, 4, 64], F32)
        lt = sb("lamt", [128, 16], F32)
        g08 = sb("g08", [128, 8], F32)
        oo = [sb("oo%d" % i, [128, 512], F32) for i in range(2)]
        lnz = [sb("lnz%d" % i, [128, 512], F32) for i in range(2)]
        rz = [sb("rz%d" % i, [128, 512], F32) for i in range(2)]
        ocmb = sb("ocmb", [128, 512], F32)
        sq = sb("sqa", [128, 512], F32)
        lnv = sb("lnva", [128, 512], F32)
        rstd = sb("rstda", [128, 512], F32)
        pS = [es.enter_context(nc.psum_tensor("pS%d" % i, [128, 1024], F32)) for i in range(2)]
        acc = [es.enter_context(nc.psum_tensor("acc%d" % i, [128, 512], F32)) for i in range(2)]
        pZ = [es.enter_context(nc.psum_tensor("pZa%d" % i, [128, 512], F32)) for i in range(2)]
        P = Phase(ctx, "ATT")
        P.add('sp', dma(kT[0][:, :], T['kT'][0, :, :]), writes=[('kT', 0)], dsem=('kT', 0))
        P.add('sp', dma(qT[0][:, :], T['qT'][0, :, :]), writes=[('qT', 0)], dsem=('qT', 0))
        P.add('dve', lambda e: e.memset(ones32[:, :], 1.0), writes=['ones32'])
        P.add('dve', lambda e: e.memset(onesb[:, :], 1.0), writes=['onesb'])
        P.add('sp', dma(kbias[:, :], T['kbias'][:, :]), writes=['kbias'], dsem='kbias')
        P.add('sp', lambda e, sm: [e.dma_start(out=lv[:, j, :], in_=T[nm][0:1, :].partition_broadcast(128)).then_inc(sm, 16)
                                   for j, nm in enumerate(('lq1', 'lk1', 'lq2', 'lk2'))],
              writes=['lv'], dsem='lv', ndma=4)
        P.add('sp', dma(g08[:, :], T['dahn_T'][:, :]), writes=['g08'], dsem='g08')
        P.add('dve', lambda e: e.tensor_tensor(out=lv[:, 0, :], in0=lv[:, 0, :], in1=lv[:, 1, :], op=ALU.mult),
              reads=['lv'], writes=['lv'])
        P.add('dve', lambda e: e.tensor_tensor(out=lv[:, 2, :], in0=lv[:, 2, :], in1=lv[:, 3, :], op=ALU.mult),
              reads=['lv'], writes=['lv'])
        P.add('dve', lambda e: e.reduce_sum(out=lt[:, 0:1], in_=lv[:, 0, :], axis=AX.X), reads=['lv'], writes=['lt0'])
        P.add('dve', lambda e: e.reduce_sum(out=lt[:, 1:2], in_=lv[:, 2, :], axis=AX.X), reads=['lv'], writes=['lt1'])
        P.add('act', lambda e: e.activation(out=lt[:, 2:4], in_=lt[:, 0:2], func=AF.Exp), reads=['lt0', 'lt1'], writes=['lt2'])
        P.add('dve', lambda e: e.tensor_tensor(out=lt[:, 4:5], in0=lt[:, 3:4], in1=lt[:, 2:3], op=ALU.subtract),
              reads=['lt2'], writes=['lt4'])
        P.add('dve', lambda e: e.tensor_scalar(out=lt[:, 5:6], in0=lt[:, 4:5], scalar1=-0.2, scalar2=None, op0=ALU.add),
              reads=['lt4'], writes=['neglam'])
        P.add('dve', lambda e: e.tensor_scalar(out=g08[:, :], in0=g08[:, :], scalar1=0.8, scalar2=None, op0=ALU.mult),
              reads=['g08'], writes=['g08'])
        neglam = lt[:, 5:6]

        def load_head(h):
            hb = h % 2
            if h > 0:
                P.add('sp', dma(kT[hb][:, :], T['kT'][h, :, :]), writes=[('kT', hb)], dsem=('kT', hb))
                P.add('sp', dma(qT[hb][:, :], T['qT'][h, :, :]), writes=[('qT', hb)], dsem=('qT', hb))
            P.add('sp', lambda e, sm: (
                e.dma_start(out=vt[hb][:, 0:32, :],
                            in_=T['V'][0:4096, h * 128:(h + 1) * 128].rearrange("(t p) d -> p t d", p=128)).then_inc(sm, 16),
                e.dma_start(out=vt[hb][:32, 32, :], in_=T['V'][4096:4128, h * 128:(h + 1) * 128]).then_inc(sm, 16)),
                writes=[('vt', hb)], dsem=('vt', hb), ndma=2)

        steps = [(h, qb, kt) for h in range(NH) for qb in range(NQB) for kt in range(nkt)]
        nsteps = len(steps)
        blk_of = lambda i: i // nkt

        def S_op(i):
            h, qb, kt = steps[i]
            hb = h % 2
            sl = i % 2
            rows = 128 if kt < 32 else 32
            k0 = kt * 128

            def f(e):
                ins = None
                for c in range(2):
                    ins = e.matmul(pS[sl][:rows, c * 512:(c + 1) * 512], lhsT=kT[hb][c * 64:(c + 1) * 64, k0:k0 + rows],
                                   rhs=qT[hb][c * 64:(c + 1) * 64, qb * 512:(qb + 1) * 512], start=True, stop=True)
                return ins
            P.add('pe', f, reads=[('kT', hb), ('qT', hb)], writes=[('pS', sl)])

        def E_op(i):
            h, qb, kt = steps[i]
            sl = i % 2
            p3 = i % NPT
            rows = 128 if kt < 32 else 32
            if kt < 32:
                P.add('act', lambda e: e.activation(out=pt[p3][:rows, :], in_=pS[sl][:rows, :], func=AF.Exp, scale=0.125),
                      reads=[('pS', sl)], writes=[('pt', p3)])
            else:
                P.add('act', lambda e: e.activation(out=pt[p3][:rows, :], in_=pS[sl][:rows, :], func=AF.Exp, scale=0.125,
                                                    bias=kbias[:, 0:1]),
                      reads=[('pS', sl), 'kbias'], writes=[('pt', p3)])

        def AV_op(i):
            h, qb, kt = steps[i]
            hb = h % 2
            p3 = i % NPT
            zb = blk_of(i) % 2
            rows = 128 if kt < 32 else 32
            st_, sp_ = (kt == 0), (kt == nkt - 1)

            def f(e):
                e.matmul(acc[0][:, :], lhsT=vt[hb][:rows, kt, :], rhs=pt[p3][:rows, 0:512], start=st_, stop=sp_)
                e.matmul(acc[1][:, :], lhsT=vt[hb][:rows, kt, :], rhs=pt[p3][:rows, 512:1024], start=st_, stop=sp_)
                return e.matmul(pZ[0][:, :], lhsT=onesb[:rows, :], rhs=pt[p3][:rows, 0:512], start=st_, stop=sp_)
            P.add('pe', f, reads=[('vt', hb), ('pt', p3), 'onesb'], writes=['acc0', 'acc1', ('pZ', 0)])
            for c, eng in ((1, 'dve'),):
                z = zacc[zb][c]
                src_ = pt[p3][:rows, c * 512:(c + 1) * 512]
                if kt == 0:
                    P.add(eng, lambda e, z=z, src_=src_: e.tensor_copy(out=z[:rows, :], in_=src_),
                          reads=[('pt', p3)], writes=[('zacc', zb, c)])
                else:
                    P.add(eng, lambda e, z=z, src_=src_: e.tensor_tensor(out=z[:rows, :], in0=z[:rows, :], in1=src_, op=ALU.add),
                          reads=[('pt', p3)], writes=[('zacc', zb, c)])

        def epiA(h, qb):
            P.add('dve', lambda e: e.tensor_copy(out=oo[0][:, :], in_=acc[0][:, :]), reads=['acc0'], writes=[('oo', 0)])
            P.add('dve', lambda e: e.tensor_copy(out=oo[1][:, :], in_=acc[1][:, :]), reads=['acc1'], writes=[('oo', 1)])
            P.add('dve', lambda e: e.tensor_copy(out=zs0[:, :], in_=pZ[0][:, :]), reads=[('pZ', 0)], writes=['zs0'])

        def epiB(h, qb, zb):
            for c in (1,):
                P.add('pe', lambda e, c=c: e.matmul(pZ[c][:, :], lhsT=ones32[:, :], rhs=zacc[zb][c][:, :], start=True, stop=True),
                      reads=['ones32', ('zacc', zb, c)], writes=[('pZ', c)])

        def epiB2(h, qb):
            for c in range(2):
                if c == 0:
                    P.add('act', lambda e, c=c: e.activation(out=lnz[c][:, :], in_=zs0[:, :], func=AF.Ln), reads=['zs0'], writes=[('lnz', c)])
                    P.add('act', lambda e, c=c: e.activation(out=rz[c][:, :], in_=lnz[c][:, :], func=AF.Exp, scale=-1.0),
                          reads=[('lnz', c)], writes=[('rz', c)])
                else:
                    P.add('act', lambda e, c=c: e.activation(out=lnz[c][:, :], in_=pZ[c][:, :], func=AF.Ln), reads=[('pZ', c)], writes=[('lnz', c)])
                    P.add('act', lambda e, c=c: e.activation(out=rz[c][:, :], in_=lnz[c][:, :], func=AF.Exp, scale=-1.0),
                          reads=[('lnz', c)], writes=[('rz', c)])
                P.add('dve', lambda e, c=c: e.tensor_tensor(out=oo[c][:, :], in0=oo[c][:, :], in1=rz[c][:, :], op=ALU.mult),
                      reads=[('rz', c)], writes=[('oo', c)])
            P.add('dve', lambda e: e.scalar_tensor_tensor(out=ocmb[:, :], in0=oo[1][:, :], scalar=neglam, in1=oo[0][:, :],
                                                          op0=ALU.mult, op1=ALU.add),
                  reads=[('oo', 0), ('oo', 1), 'neglam'], writes=['ocmb'])
            P.add('dve', lambda e: e.tensor_tensor(out=sq[:, :], in0=ocmb[:, :], in1=ocmb[:, :], op=ALU.mult),
                  reads=['ocmb'], writes=['sq'])

        def epiC(h, qb):
            P.add('pe', lambda e: e.matmul(pZ[1][:, :], lhsT=ones32[:, :], rhs=sq[:, :], start=True, stop=True),
                  reads=['ones32', 'sq'], writes=[('pZ', 1)])

        def epiC2(h, qb):
            P.add('act', lambda e: e.activation(out=lnv[:, :], in_=pZ[1][:, :], func=AF.Ln, scale=1.0 / 128, bias=EPS),
                  reads=[('pZ', 1)], writes=['lnv'])
            P.add('act', lambda e: e.activation(out=rstd[:, :], in_=lnv[:, :], func=AF.Exp, scale=-0.5),
                  reads=['lnv'], writes=['rstd'])
            P.add('dve', lambda e: e.scalar_tensor_tensor(out=mixedT[:, h, qb * 512:(qb + 1) * 512], in0=ocmb[:, :],
                                                          scalar=g08[:, h:h + 1], in1=rstd[:, :], op0=ALU.mult, op1=ALU.mult),
                  reads=['ocmb', 'rstd', 'g08'], writes=[('mixedT', h, qb)])
            if qb == NQB - 1:
                P.add('sp', dma(T['mixT'][:, h, 0:NQB * 512], mixedT[:, h, 0:NQB * 512]),
                      reads=[('mixedT', h, q_) for q_ in range(NQB)], dsem=('dump', h % 2))

        deferred = {}

        def defer(step, fn):
            deferred.setdefault(step, []).append(fn)

        load_head(0)
        S_op(0)
        S_op(1)
        for i in range(nsteps):
            h, qb, kt = steps[i]
            if kt == 0 and qb == 0 and h + 1 < NH:
                load_head(h + 1)
            E_op(i)
            if i + 2 < nsteps:
                S_op(i + 2)
            AV_op(i)
            for fn in deferred.pop(i, []):
                fn()
            if kt == nkt - 1:
                zb = blk_of(i) % 2
                epiA(h, qb)
                defer(i + 2, lambda h=h, qb=qb, zb=zb: epiB(h, qb, zb))
                defer(i + 3, lambda h=h, qb=qb: epiB2(h, qb))
                defer(i + 7, lambda h=h, qb=qb: epiC(h, qb))
                defer(i + 8, lambda h=h, qb=qb: epiC2(h, qb))
        for k in sorted(deferred):
            for fn in deferred[k]:
                fn()
        P.run()


def phase_ML(nc, ctx, T):
    with ExitStack() as es:
        def sb(name, shape, dt):
            return es.enter_context(nc.sbuf_tensor(name, list(shape), dt))
        qTm = sb("qTm", [128, 4, NOWN], BF16)
        kTo = sb("kTo", [128, 4, NOWN], BF16)
        ktm = sb("ktm", [128, NTILE, 512], BF16)
        cf = sb("cfml", [128, 4, 128], F32)
        identb = sb("identml", [128, 128], BF16)
        ones32 = sb("ones32ml", [128, 128], F32)
        EB = sb("EB", [128, NTILE, 8], F32)
        AA = sb("AAml", [128, NTILE, 8], F32)
        WK = sb("WKml", [128, NTILE, 8], F32)
        DEC = sb("DECml", [128, 32, 8], F32)
        U32, L32, Ublk, Lblk = cf[:, 0, :], cf[:, 1, :], cf[:, 2, :], cf[:, 3, :]

        with ExitStack() as es1:
            def sb1(name, shape, dt):
                return es1.enter_context(nc.sbuf_tensor(name, list(shape), dt))
            g = sb1("gml", [128, NTILE, 16], F32)
            gb = sb1("gbml", [128, 16], F32)
            mb = sb1("mbml", [32, 2], F32)
            SP = sb1("SPml", [128, NTILE, 8], F32)
            X1 = sb1("X1ml", [128, NTILE, 8], F32)
            X2 = sb1("X2ml", [128, NTILE, 8], F32)
            LI = sb1("LIml", [128, NTILE, 8], F32)
            TA = sb1("TAml", [128, NTILE, 8], F32)
            TB = sb1("TBml", [128, NTILE, 8], F32)
            raw = [sb1("rawml%d" % i, [128, SEQW], BF16) for i in range(2)]
            identf = sb1("identf", [128, 128], F32)
            dg = sb1("dgml", [128, 40, 128], BF16)
            kfull = [sb1("kfull%d" % i, [128, NT], BF16) for i in range(2)]
            qsil = sb1("qsil", [128, NOWN], BF16)
            mst = sb1("mstml", [128, 32], BF16)
            cw = sb1("cwml", [128, 8, 5], F32)
            cb = sb1("cbml", [128, 8], F32)
            pP = es1.enter_context(nc.psum_tensor("pPml", [128, 512], F32))
            pSx = es1.enter_context(nc.psum_tensor("pSxml", [128, 512], F32))
            pTt = es1.enter_context(nc.psum_tensor("pTtml", [128, 512], F32))
            pM = es1.enter_context(nc.psum_tensor("pMml", [128, 512], F32))
            ptr = [es1.enter_context(nc.psum_tensor("ptrml%d" % i, [128, 1024], BF16)) for i in range(2)]
            pcv = [es1.enter_context(nc.psum_tensor("pcvml%d" % i, [128, 512], F32)) for i in range(2)]
            P = Phase(ctx, "ML1")
            P.add('pool', dma(identb[:, :], T['consts'][:, 0:128]), writes=['identb'], dsem='identb')
            P.add('sp', dma(cf[:, :, :], T['consts'][:, 256:768].rearrange("p (a b) -> p a b", a=4)), writes=['cf'], dsem='cf')
            P.add('dve', lambda e: e.memset(ones32[:, :], 1.0), writes=['ones32'])
            P.add('dve', lambda e: e.memset(g[:, :, :], 0.0), writes=['g'])
            P.add('sp', lambda e, sm: (
                e.dma_start(out=g[:, 0:32, :], in_=T['MG'][0:4096, :].rearrange("(t p) c -> p t c", p=128)).then_inc(sm, 16),
                e.dma_start(out=g[:32, 32, :], in_=T['MG'][4096:4128, :]).then_inc(sm, 16)),
                writes=['g'], dsem='g', ndma=2)
            P.add('sp', dma(gb[:, :], T['gbias'][0:1, :].partition_broadcast(128)), writes=['gb'], dsem='gb')
            P.add('sp', dma(mb[:, :], T['mbias'][:, :]), writes=['mb'], dsem='mb')
            P.add('sp', dma(cw[:, :, :], T['convw_T'][:, :].rearrange("p (c k) -> p c k", c=8)), writes=['cw'], dsem='cw')
            P.add('sp', dma(cb[:, :], T['convb_T'][:, :]), writes=['cb'], dsem='cb')
            P.add('dve', lambda e: e.tensor_tensor(out=g[:, :, :], in0=g[:, :, :],
                                                   in1=gb[:, :].unsqueeze(1).to_broadcast([128, NTILE, 16]), op=ALU.add),
                  reads=['gb'], writes=['g'])
            g5 = g[:, :, :].rearrange("p t (d k h) -> p t d k h", d=2, k=2, h=4)
            gi = g5[:, :, :, 0, :]
            gf = g5[:, :, :, 1, :]
            SP4 = SP[:, :, :].rearrange("p t (d h) -> p t d h", d=2)
            LI4 = LI[:, :, :].rearrange("p t (d h) -> p t d h", d=2)
            P.add('act', lambda e: e.activation(out=SP4, in_=gf, func=AF.Exp, scale=-1.0), reads=['g'], writes=['SP'])
            P.add('act', lambda e: e.activation(out=SP[:, :, :], in_=SP[:, :, :], func=AF.Ln, bias=1.0), reads=['SP'], writes=['SP'])
            P.add('dve', lambda e: e.tensor_copy(out=LI4, in_=gi), reads=['g'], writes=['LI'])
            P.add('dve', lambda e: e.tensor_tensor(out=LI4[:32, 32, :, :], in0=LI4[:32, 32, :, :],
                                                   in1=mb[:, :].unsqueeze(2).to_broadcast([32, 2, 4]), op=ALU.add),
                  reads=['mb'], writes=['LI'])
            rhs_all = SP[:, 0:32, :].rearrange("p t c -> p (t c)")
            P.add('pe', lambda e: e.matmul(pP[:, 0:256], lhsT=U32, rhs=rhs_all, start=True, stop=True), reads=['SP', 'cf'], writes=['pP'])
            P.add('pe', lambda e: e.matmul(pSx[:, 0:256], lhsT=L32, rhs=rhs_all, start=True, stop=True), reads=['SP', 'cf'], writes=['pSx'])
            P.add('pe', lambda e: e.matmul(pTt[:, 0:256], lhsT=ones32[:, :], rhs=rhs_all, start=True, stop=True), reads=['SP', 'ones32'], writes=['pTt'])
            P.add('pe', lambda e: (e.matmul(pM[:32, 0:8], lhsT=Ublk[:32, :32], rhs=SP[:32, 32, :], start=True, stop=True),
                                   e.matmul(pM[:32, 8:16], lhsT=Lblk[:32, :32], rhs=SP[:32, 32, :], start=True, stop=True))[1],
                  reads=['SP', 'cf'], writes=['pM'])
            pP3 = pP[:, 0:256].rearrange("p (t c) -> p t c", c=8)
            pS3 = pSx[:, 0:256].rearrange("p (t c) -> p t c", c=8)
            pT3 = pTt[:, 0:256].rearrange("p (t c) -> p t c", c=8)
            P.add('dve', lambda e: e.tensor_copy(out=X1[:, 0:32, 0:4], in_=pP3[:, :, 0:4]), reads=['pP'], writes=['X1a'])
            P.add('dve', lambda e: e.tensor_copy(out=X1[:, 0:32, 4:8], in_=pS3[:, :, 4:8]), reads=['pSx'], writes=['X1b'])
            P.add('dve', lambda e: e.tensor_copy(out=X2[:, 0:32, 0:4], in_=pS3[:, :, 0:4]), reads=['pSx'], writes=['X2a'])
            P.add('dve', lambda e: e.tensor_copy(out=X2[:, 0:32, 4:8], in_=pP3[:, :, 4:8]), reads=['pP'], writes=['X2b'])
            P.add('dve', lambda e: e.tensor_copy(out=X2[:32, 32, 0:4], in_=pM[:32, 8:12]), reads=['pM'], writes=['X2c'])
            P.add('dve', lambda e: e.tensor_copy(out=X2[:32, 32, 4:8], in_=pM[:32, 4:8]), reads=['pM'], writes=['X2d'])
            P.add('act', lambda e: e.activation(out=DEC[:, :, :], in_=pT3, func=AF.Exp, scale=-1.0), reads=['pTt'], writes=['DEC'])
            P.add('act', lambda e: e.activation(out=EB[:, 0:32, :], in_=X1[:, 0:32, :], func=AF.Exp, scale=-1.0),
                  reads=['X1a', 'X1b'], writes=['EB'])
            P.add('dve', lambda e: e.tensor_tensor(out=TA[:, 0:32, :], in0=LI[:, 0:32, :], in1=X1[:, 0:32, :], op=ALU.add),
                  reads=['LI', 'X1a', 'X1b'], writes=['TA'])
            P.add('act', lambda e: e.activation(out=AA[:, 0:32, :], in_=TA[:, 0:32, :], func=AF.Exp), reads=['TA'], writes=['AA'])
            P.add('dve', lambda e: e.tensor_tensor(out=TB[:, 0:32, :], in0=LI[:, 0:32, :], in1=X2[:, 0:32, :], op=ALU.subtract),
                  reads=['LI', 'X2a', 'X2b'], writes=['TB'])
            P.add('dve', lambda e: e.tensor_tensor(out=TB[:32, 32, :], in0=LI[:32, 32, :], in1=X2[:32, 32, :], op=ALU.subtract),
                  reads=['LI', 'X2c', 'X2d', 'TB'], writes=['TB'])
            P.add('dve', lambda e: e.tensor_tensor(out=TB[:, 0:32, :], in0=TB[:, 0:32, :], in1=SP[:, 0:32, :], op=ALU.add),
                  reads=['SP', 'TB'], writes=['TB'])
            P.add('dve', lambda e: e.tensor_tensor(out=TB[:32, 32, :], in0=TB[:32, 32, :], in1=SP[:32, 32, :], op=ALU.add),
                  reads=['SP', 'TB'], writes=['TB'])
            P.add('act', lambda e: e.activation(out=WK[:, 0:32, :], in_=TB[:, 0:32, :], func=AF.Exp), reads=['TB'], writes=['WK'])
            P.add('act', lambda e: e.activation(out=WK[:32, 32, :], in_=TB[:32, 32, :], func=AF.Exp), reads=['TB', 'WK'], writes=['WK'])

            P.add('sp', dma(identf[:, :], T['consts'][:, 0:128]), writes=['identf'], dsem='identf')
            for cc in range(8):
                for j in range(5):
                    P.add('dve', lambda e, cc=cc, j=j: e.tensor_scalar(out=dg[:, cc * 5 + j, :], in0=identf[:, :], scalar1=cw[:, cc, j:j + 1],
                                                                       scalar2=None, op0=ALU.mult),
                          reads=['identf', 'cw'], writes=[('dg', cc)])
            for s in range(2):
                P.add('pool', lambda e, s=s: e.memset(raw[s][:, 0:2], 0.0), writes=[('rawh0', s)])
                P.add('pool', lambda e, s=s: e.memset(raw[s][:, SEQW - 2:SEQW], 0.0), writes=[('rawh1', s)])
            ccnt = {'p': 0}
            for cc in range(8):
                s = cc % 2
                isq = cc < 4
                off, W = (16, NOWN) if isq else (0, NT)
                if isq:
                    P.add("sp", dma(raw[s][:, 2:2098], T["MQK"][cc * 128:(cc + 1) * 128, 2:2098]),
                          writes=[('raw', s)], dsem=('raw', s))
                else:
                    P.add('sp', dma(raw[s][:, 2:SEQW - 2], T['MQK'][cc * 128:(cc + 1) * 128, 2:SEQW - 2]),
                          writes=[('raw', s)], dsem=('raw', s))
                h = cc if isq else cc - 4
                kb_ = h % 2
                kf = kfull[kb_]
                nblk = (W + 511) // 512
                for b in range(nblk):
                    n0 = b * 512
                    nw = min(512, W - n0)
                    pb = ccnt['p'] % 2
                    ccnt['p'] += 1

                    def mm(e, s=s, cc=cc, off=off, n0=n0, nw=nw, pb=pb):
                        ins = None
                        for j in range(5):
                            ins = e.matmul(pcv[pb][:, 0:nw], lhsT=dg[:, cc * 5 + j, :], rhs=raw[s][:, off + j + n0:off + j + n0 + nw],
                                           start=(j == 0), stop=(j == 4))
                        return ins
                    P.add('pe', mm, reads=[('raw', s), ('rawh0', s), ('rawh1', s), ('dg', cc)], writes=[('pcv', pb)])
                    if isq:
                        P.add('act', lambda e, cc=cc, n0=n0, nw=nw, pb=pb: e.activation(out=qsil[:, n0:n0 + nw], in_=pcv[pb][:, 0:nw],
                                                                                      func=AF.Silu, bias=cb[:, cc:cc + 1]),
                              reads=[('pcv', pb), 'cb'], writes=[('qsil', b)])
                    else:
                        P.add('act', lambda e, cc=cc, n0=n0, nw=nw, pb=pb, kf=kf: e.activation(out=kf[:, n0:n0 + nw], in_=pcv[pb][:, 0:nw],
                                                                                             func=AF.Silu, bias=cb[:, cc:cc + 1]),
                              reads=[('pcv', pb), 'cb'], writes=[('kfull', kb_, b)])
                if isq:
                    P.add('pool', lambda e, h=h: e.tensor_scalar(out=qTm[:, h, :], in0=qsil[:, :], scalar1=float(128 ** -0.5),
                                                                 scalar2=1.0, op0=ALU.mult, op1=ALU.mult),
                          reads=[('qsil', b) for b in range(nblk)], writes=[('qTm', h)])
                else:
                    kall = [('kfull', kb_, b) for b in range(nblk)]
                    P.add('pool', lambda e, h=h, kf=kf: e.tensor_copy(out=kTo[:, h, :], in_=kf[:, 16:16 + NOWN]),
                          reads=kall, writes=[('kTo', h)])
                    P.add('pool', lambda e, kf=kf: e.tensor_copy(out=mst[:, 0:16], in_=kf[:, 0:16]),
                          reads=kall, writes=['mst0'])
                    P.add('pool', lambda e, kf=kf: e.tensor_copy(out=mst[:, 16:32], in_=kf[:, 4112:4128]),
                          reads=kall, writes=['mst1'])
                    for grp in range(4):
                        pb = grp % 2

                        def tr(e, kf=kf, grp=grp, pb=pb):
                            ins = None
                            for j in range(8):
                                tt = grp * 8 + j
                                ins = e.transpose(out=ptr[pb][:, j * 128:(j + 1) * 128],
                                                  in_=kf[:, 16 + tt * 128:16 + (tt + 1) * 128], identity=identb[:, :])
                            return ins
                        P.add('pe', tr, reads=kall + ['identb'], writes=[('ptr', pb)])
                        eng = 'act' if grp % 2 == 0 else 'dve'

                        def ev(e, h=h, grp=grp, pb=pb, eng=eng):
                            src_ = ptr[pb][:, :].rearrange("p (t d) -> p t d", t=8)
                            dst = ktm[:, grp * 8:(grp + 1) * 8, h * 128:(h + 1) * 128]
                            if eng == 'act':
                                return e.copy(out=dst, in_=src_)
                            return e.tensor_copy(out=dst, in_=src_)
                        P.add(eng, ev, reads=[('ptr', pb)], writes=[('ktm', h, grp)])
                    P.add('pe', lambda e: e.transpose(out=ptr[0][:32, 0:128], in_=mst[:, 0:32], identity=identb[:, :]),
                          reads=['mst0', 'mst1', 'identb'], writes=[('ptr', 0)])
                    P.add('dve', lambda e, h=h: e.tensor_copy(out=ktm[:32, 32, h * 128:(h + 1) * 128], in_=ptr[0][:32, 0:128]),
                          reads=[('ptr', 0)], writes=[('ktm', h, 4)])
            P.run()

        with ExitStack() as es2:
            def sb2(name, shape, dt):
                return es2.enter_context(nc.sbuf_tensor(name, list(shape), dt))
            C32 = sb2("C32", [128, 8, 257], F32)
            Cbf = sb2("Cbf", [128, 8, 257], BF16)
            NV = 6
            v1 = [sb2("v1_%d" % i, [128, 4, 257], BF16) for i in range(NV)]
            Sm = [sb2("Sm%d" % i, [128, 128], BF16) for i in range(4)]
            kbf = [sb2("kbf%d" % i, [128, 128], BF16) for i in range(4)]
            dn = sb2("dnml", [128, 8, 4], F32)
            hxu = sb2("hxu", [128, 8, 256], F32)
            hX = [[sb2("hX%d_%d" % (x, i), [128, 1024], F32) for i in range(2)] for x in range(2)]
            hAt = [[sb2("hAt%d_%d" % (f, i), [128, 1024], F32) for i in range(1)] for f in range(2)]
            hBt = [[sb2("hBt%d_%d" % (f, i), [128, 1024], F32) for i in range(1)] for f in range(2)]
            mo = [[sb2("mo%d_%d" % (f, i), [128, 1024], F32) for i in range(1)] for f in range(2)]
            hs = [sb2("hsml%d" % f, [128, 1024], F32) for f in range(2)]
            junk = [sb2("junkml%d" % f, [128, 256], BF16) for f in range(2)]
            ssq = [sb2("ssqml%d" % f, [128, 12], F32) for f in range(2)]
            hn = [sb2("hnml%d" % f, [128, 1024], F32) for f in range(2)]
            og = [sb2("ogml%d" % f, [128, 1024], F32) for f in range(2)]
            mx = [sb2("mxml%d" % f, [128, 1024], BF16) for f in range(2)]
            mlT = [[sb2("mlT%d_%d" % (f, i), [128, 8, 128], BF16) for i in range(1)] for f in range(2)]
            cnt_f = [0, 0]
            gbc = sb2("gbcml", [128, 1024], F32)
            mhalf = sb2("mhalf", [128, 4], F32)
            pSt = [es2.enter_context(nc.psum_tensor("pSt%d" % i, [128, 512], F32)) for i in range(2)]
            pH = [es2.enter_context(nc.psum_tensor("pH%d" % i, [128, 512], F32)) for i in range(2)]
            pC = [es2.enter_context(nc.psum_tensor("pC%d" % i, [128, 512], F32)) for i in range(2)]
            pX = [es2.enter_context(nc.psum_tensor("pXml%d" % f, [128, 1024], BF16)) for f in range(2)]
            P = Phase(ctx, "ML2")
            P.add('dve', lambda e: e.memset(C32[:, :, :], 0.0), writes=[('C32', i) for i in range(8)])
            P.add('pool', lambda e: e.memset(Cbf[:, :, :], 0.0), writes=[('Cbf', i) for i in range(8)])
            for i in range(NV):
                P.add('pool', lambda e, i=i: e.memset(v1[i][:, :, 256:257], 1.0), writes=[('v1one', i)])
            P.add('sp', dma(gbc[:, :], T['mlhn'][0:1, :].partition_broadcast(128)), writes=['gbc'], dsem='gbc')
            P.add('dve', lambda e: e.tensor_scalar(out=gbc[:, :], in0=gbc[:, :], scalar1=0.5, scalar2=None, op0=ALU.mult),
                  reads=['gbc'], writes=['gbc'])
            P.add('pool', lambda e: e.memset(mhalf[:, :], -0.5), writes=['mhalf'])
            cnt = {'v': 0, 'k': 0, 'c': 0, 's': 0, 'h': 0, 'x': 0}

            def load_v(tt):
                rows = 128 if tt < 32 else 32
                i = cnt['v'] % NV
                cnt['v'] += 1
                P.add('sp', dma(v1[i][:rows, :, 0:256], T['MV'][tt * 128:tt * 128 + rows, :].rearrange("p (h d) -> p h d", h=4)),
                      writes=[('v1', i)], dsem=('v1', i))
                return i

            def st_kb(tt, h, X, ctxd):
                rows = 128 if tt < 32 else 32
                ci = X * 4 + h
                ki = cnt['k'] % 4
                cnt['k'] += 1
                ctxd[('ki', h)] = ki
                P.add('act', lambda e: e.activation(out=kbf[ki][:rows, :], in_=ktm[:rows, tt, h * 128:(h + 1) * 128], func=AF.Copy,
                                                    scale=WK[:rows, tt, ci:ci + 1]),
                      reads=[('ktm', h, min(tt // 8, 4)), 'WK'], writes=[('kbf', ki)])

            def st_dc(tt, h, X, ctxd):
                rows = 128 if tt < 32 else 32
                ki = ctxd[('ki', h)]
                vi = ctxd['vi']
                pc = cnt['c'] % 2
                cnt['c'] += 1
                ctxd[('pc', h)] = pc
                P.add('pe', lambda e: e.matmul(pC[pc][:, 0:257], lhsT=kbf[ki][:rows, :], rhs=v1[vi][:rows, h, :], start=True, stop=True),
                      reads=[('kbf', ki), ('v1', vi), ('v1one', vi)], writes=[('pC', pc)])

            def st_c32(tt, h, X, ctxd):
                ci = X * 4 + h
                pc = ctxd[('pc', h)]
                if tt < 32:
                    P.add('dve', lambda e: e.scalar_tensor_tensor(out=C32[:, ci, :], in0=C32[:, ci, :], scalar=DEC[:, tt, ci:ci + 1],
                                                                  in1=pC[pc][:, 0:257], op0=ALU.mult, op1=ALU.add),
                          reads=[('pC', pc), 'DEC'], writes=[('C32', ci)])
                else:
                    P.add('dve', lambda e: e.tensor_copy(out=C32[:, ci, :], in_=pC[pc][:, 0:257]),
                          reads=[('pC', pc)], writes=[('C32', ci)])

            def st_cast(tt, h, X, ctxd):
                ci = X * 4 + h
                P.add('act', lambda e: e.copy(out=Cbf[:, ci, :], in_=C32[:, ci, :]), reads=[('C32', ci)], writes=[('Cbf', ci)])

            def o_st(tt, h, X, ctxd):
                t0 = tt * 128
                si = cnt['s'] % 2
                cnt['s'] += 1
                ctxd[('si', h)] = si
                P.add('pe', lambda e: e.matmul(pSt[si][:, 0:128], lhsT=kTo[:, h, t0:t0 + 128], rhs=qTm[:, h, t0:t0 + 128],
                                               start=True, stop=True),
                      reads=[('kTo', h), ('qTm', h)], writes=[('pSt', si)])

            def o_sm(tt, h, X, ctxd):
                ci = X * 4 + h
                si = ctxd[('si', h)]
                mask = U32 if X == 0 else L32
                mi = cnt['x'] % 4
                cnt['x'] += 1
                ctxd[('mi', h)] = mi
                P.add('dve', lambda e: e.scalar_tensor_tensor(
                    out=Sm[mi][:, :], in0=pSt[si][:, 0:128], scalar=AA[:, tt, ci:ci + 1], in1=mask, op0=ALU.mult, op1=ALU.mult),
                    reads=[('pSt', si), 'AA', 'cf'], writes=[('Sm', mi)])

            def o_mmh(tt, h, X, ctxd):
                t0 = tt * 128
                ci = X * 4 + h
                mi = ctxd[('mi', h)]
                vi = ctxd['vi']
                hi = cnt['h'] % 2
                cnt['h'] += 1
                ctxd[('hi', h)] = hi

                def mmH(e):
                    e.matmul(pH[hi][:, 0:257], lhsT=Sm[mi][:, :], rhs=v1[vi][:, h, :], start=True, stop=False)
                    return e.matmul(pH[hi][:, 0:257], lhsT=qTm[:, h, t0:t0 + 128], rhs=Cbf[:, ci, :], start=False, stop=True)
                P.add('pe', mmH, reads=[('Sm', mi), ('v1', vi), ('v1one', vi), ('qTm', h), ('Cbf', ci)], writes=[('pH', hi)])

            def o_evac(tt, h, X, ctxd):
                ci = X * 4 + h
                hi = ctxd[('hi', h)]
                P.add('act', lambda e: e.activation(out=dn[:, ci, 3:4], in_=pH[hi][:, 256:257], func=AF.Abs, scale=EB[:, tt, ci:ci + 1]),
                      reads=[('pH', hi), 'EB'], writes=[('dn3', ci)])
                P.add('act', lambda e: e.copy(out=hxu[:, ci, :], in_=pH[hi][:, 0:256]), reads=[('pH', hi)], writes=[('hxu', ci)])

            def o_dn(tt, h, X, ctxd):
                if h != 3:
                    return
                c0 = X * 4
                cis = range(c0, c0 + 4)
                P.add('dve', lambda e: e.tensor_scalar(out=dn[:, c0:c0 + 4, 0], in0=dn[:, c0:c0 + 4, 3], scalar1=1.0, scalar2=None, op0=ALU.max),
                      reads=[('dn3', ci) for ci in cis], writes=[('dn0', ci) for ci in cis])
                P.add('dve', lambda e: e.reciprocal(out=dn[:, c0:c0 + 4, 1], in_=dn[:, c0:c0 + 4, 0]),
                      reads=[('dn0', ci) for ci in cis], writes=[('dn1', ci) for ci in cis])
                P.add('dve', lambda e: e.tensor_tensor(out=dn[:, c0:c0 + 4, 2], in0=dn[:, c0:c0 + 4, 1], in1=EB[:, tt, c0:c0 + 4], op=ALU.mult),
                      reads=[('dn1', ci) for ci in cis] + ['EB'], writes=[('dn2', ci) for ci in cis])

            def o_hx(tt, h, X, ctxd):
                ci = X * 4 + h
                hx = ctxd['hx']
                hk = ctxd['hxkey']
                P.add('pool', lambda e: e.tensor_scalar(out=hx[:, h * 256:(h + 1) * 256], in0=hxu[:, ci, :], scalar1=dn[:, ci, 2:3],
                                                        scalar2=1.0, op0=ALU.mult, op1=ALU.mult),
                      reads=[('hxu', ci), ('dn2', ci)], writes=[(hk, h)])

            OUT_STAGES = [(o_st, 0), (o_sm, 1), (st_kb, 1), (o_mmh, 2), (o_evac, 3), (st_dc, 3), (o_dn, 4), (st_c32, 4),
                          (st_cast, 5), (o_hx, 8)]
            STATE_STAGES = [(st_kb, 0), (st_dc, 1), (st_c32, 2)]

            def wave(stages, tt, X, ctxd):
                last = max(off for _, off in stages) + 3
                for s in range(last + 1):
                    for fn, off in reversed(stages):
                        h = s - off
                        if 0 <= h < 4:
                            fn(tt, h, X, ctxd)
                    yield

            def scan_gen(X, tiles, dst):
                vnext = load_v(tiles[0])
                for j, tt in enumerate(tiles):
                    par = j % 2
                    hx = hX[X][par]
                    vcur = vnext
                    if j + 1 < len(tiles):
                        vnext = load_v(tiles[j + 1])
                    ctxd = {'vi': vcur, 'hx': hx, 'hxkey': ('hx', X, par)}
                    for _ in wave(OUT_STAGES, tt, X, ctxd):
                        yield
                    P.add('sp', dma(dst[tt * 128:(tt + 1) * 128, :], hx[:, :]),
                          reads=[(('hx', X, par), h) for h in range(4)], writes=[('h_dram', X, tt)], dsem=('hxo', X, par))
                    done[X].add(tt)
                    yield

            fin_q = []
            fin_cnt = {'n': 0}

            def fin_gen(f):
                hs_, hn_, og_, mx_, ssq_, pX_ = hs[f], hn[f], og[f], mx[f], ssq[f], pX[f]
                k = lambda name: (name, f)
                while True:
                    if not fin_q:
                        if len(done[0]) == 16 and len(done[1]) == 16 and fin_cnt['n'] == 16:
                            return
                        yield
                        continue
                    tt = fin_q.pop(0)
                    fin_cnt['n'] += 1
                    cnt_f[f] += 1
                    s = 0
                    P.add('sp', dma(hAt[f][s][:, :], T['HA'][tt * 128:(tt + 1) * 128, :]),
                          reads=[('h_dram', 0, tt)], writes=[('hAt', f, s)], dsem=('hAt', f, s))
                    P.add('sp', dma(hBt[f][s][:, :], T['HB'][tt * 128:(tt + 1) * 128, :]),
                          reads=[('h_dram', 1, tt)], writes=[('hBt', f, s)], dsem=('hBt', f, s))
                    P.add('sp', dma(mo[f][s][:, :], T['MO'][tt * 128:(tt + 1) * 128, :]), writes=[('mo', f, s)], dsem=('mo', f, s))
                    for _ in range(5):
                        yield
                    P.add('dve', lambda e: e.tensor_tensor(out=hs_[:, :], in0=hAt[f][s][:, :], in1=hBt[f][s][:, :], op=ALU.add),
                          reads=[('hAt', f, s), ('hBt', f, s)], writes=[k('hs')])
                    P.add('act', lambda e: e.activation(out=og_[:, :], in_=mo[f][s][:, :], func=AF.Tanh, scale=0.5),
                          reads=[('mo', f, s)], writes=[k('og')])
                    yield
                    for h in range(4):
                        P.add('act', lambda e, h=h: e.activation(out=junk[f][:, :], in_=hs_[:, h * 256:(h + 1) * 256], func=AF.Square,
                                                                 accum_out=ssq_[:, h:h + 1]),
                              reads=[k('hs')], writes=[k('junk'), (k('ssq'), h)])
                    yield
                    P.add('dve', lambda e: e.tensor_scalar(out=ssq_[:, 4:8], in0=ssq_[:, 0:4], scalar1=1.0 / 256, scalar2=EPS,
                                                           op0=ALU.mult, op1=ALU.add),
                          reads=[(k('ssq'), h) for h in range(4)], writes=[k('ssq4')])
                    P.add('pool', lambda e: e.tensor_tensor(out=ssq_[:, 8:12], in0=ssq_[:, 4:8], in1=mhalf[:, :], op=ALU.pow),
                          reads=[k('ssq4'), 'mhalf'], writes=[k('ssq8')])
                    yield
                    for h in range(4):
                        P.add('dve', lambda e, h=h: e.scalar_tensor_tensor(
                            out=hn_[:, h * 256:(h + 1) * 256], in0=hs_[:, h * 256:(h + 1) * 256], scalar=ssq_[:, 8 + h:9 + h],
                            in1=gbc[:, h * 256:(h + 1) * 256], op0=ALU.mult, op1=ALU.mult),
                            reads=[k('hs'), k('ssq8'), 'gbc'], writes=[(k('hn'), h)])
                    P.add('dve', lambda e: e.scalar_tensor_tensor(out=mx_[:, :], in0=og_[:, :], scalar=1.0, in1=hn_[:, :],
                                                                  op0=ALU.add, op1=ALU.mult),
                          reads=[(k('hn'), h) for h in range(4)] + [k('og')], writes=[k('mx')])
                    yield

                    def tr(e):
                        ins = None
                        for j in range(8):
                            ins = e.transpose(out=pX_[:, j * 128:(j + 1) * 128], in_=mx_[:, j * 128:(j + 1) * 128], identity=identb[:, :])
                        return ins
                    P.add('pe', tr, reads=[k('mx'), 'identb'], writes=[k('pX')])
                    yield
                    P.add('act', lambda e: e.copy(out=mlT[f][s][:, :, :], in_=pX_[:, :].rearrange("p (c t) -> p c t", c=8)),
                          reads=[k('pX')], writes=[('mlT', f, s)])
                    P.add('sp', dma(T['mixT'][:, 8:16, tt * 128:(tt + 1) * 128], mlT[f][s][:, :, :]), reads=[('mlT', f, s)],
                          dsem=('mlTo', f, s))
                    yield

            pre = [(32, 1), (32, 0)] + [(tt, 1) for tt in range(31, 15, -1)]
            for (tt, X) in pre:
                ctxd = {'vi': load_v(tt)}
                for _ in wave(STATE_STAGES, tt, X, ctxd):
                    pass
            for ci in range(8):
                P.add('act', lambda e, ci=ci: e.copy(out=Cbf[:, ci, :], in_=C32[:, ci, :]), reads=[('C32', ci)], writes=[('Cbf', ci)])
            done = {0: set(), 1: set()}
            queued = set()
            gens = [scan_gen(0, list(range(16)), T['HA']), scan_gen(1, list(range(15, -1, -1)), T['HB']), fin_gen(0), fin_gen(1)]
            alive = [True, True, True, True]
            while any(alive):
                for gi, g in enumerate(gens):
                    if not alive[gi]:
                        continue
                    try:
                        next(g)
                    except StopIteration:
                        alive[gi] = False
                for tt in range(16):
                    if tt in done[0] and tt in done[1] and tt not in queued:
                        queued.add(tt)
                        fin_q.append(tt)
            P.run()


def norm_transpose(nc, ctx, name, tiles, gain_ap, cst_ap, uT, pre_ops=None):
    with ExitStack() as es1:
        def sb1(nm, shape, dt):
            return es1.enter_context(nc.sbuf_tensor(nm + name, list(shape), dt))
        xb = [sb1("xb%d" % i, [128, D], F32) for i in range(3)]
        xn = [sb1("xn%d" % i, [128, D], BF16) for i in range(3)]
        junk = sb1("junk", [128, D], BF16)
        gbc = sb1("gbc", [128, D], F32)
        ident = sb1("ident", [128, 128], BF16)
        st = [sb1("st%d" % i, [128, 4], F32) for i in range(3)]
        pT = [es1.enter_context(nc.psum_tensor("pT%d%s" % (i, name), [128, 1024], BF16)) for i in range(6)]
        P = Phase(ctx, name)
        P.add('sp', dma(gbc[:, :], gain_ap.partition_broadcast(128)), writes=['gbc'], dsem='gbc')
        P.add('pool', dma(ident[:, :], cst_ap[:, 0:128]), writes=['cst'], dsem='cst')
        if pre_ops is not None:
            pre_ops(P)
        NB = 3

        def stA(ti):
            src_ap, rows, col0 = tiles[ti]
            s = ti % NB
            P.add('sp', dma(xb[s][:rows, :], src_ap), writes=[('xb', s)], dsem=('xb', s))

        def stB(ti):
            src_ap, rows, col0 = tiles[ti]
            s = ti % NB
            P.add('act', lambda e: e.activation(
                out=junk[:rows, :], in_=xb[s][:rows, :], func=AF.Square, accum_out=st[s][:rows, 0:1]),
                reads=[('xb', s)], writes=['junk', ('st0', s)])
            P.add('act', lambda e: e.activation(
                out=st[s][:rows, 1:2], in_=st[s][:rows, 0:1], func=AF.Sqrt, scale=1.0 / D, bias=EPS),
                reads=[('st0', s)], writes=[('st1', s)])
            P.add('dve', lambda e: e.reciprocal(out=st[s][:rows, 2:3], in_=st[s][:rows, 1:2]),
                  reads=[('st1', s)], writes=[('st2', s)])
            P.add('dve', lambda e: e.scalar_tensor_tensor(
                out=xn[s][:rows, :], in0=xb[s][:rows, :], scalar=st[s][:rows, 2:3], in1=gbc[:rows, :],
                op0=ALU.mult, op1=ALU.mult),
                reads=[('xb', s), ('st2', s), 'gbc'], writes=[('xn', s)])

        def stC(ti):
            src_ap, rows, col0 = tiles[ti]
            s = ti % NB
            for half in range(2):
                pb = (ti % NB) * 2 + half

                def tr(e, half=half, pb=pb):
                    ins = None
                    for j in range(8):
                        c = half * 8 + j
                        ins = e.transpose(out=pT[pb][:, j * 128:j * 128 + rows],
                                          in_=xn[s][:rows, c * 128:(c + 1) * 128], identity=ident[:rows, :rows])
                    return ins
                P.add('pe', tr, reads=[('xn', s), 'cst'], writes=[('pT', pb)])
                eng = 'act' if half == 0 else 'dve'

                def ev(e, half=half, pb=pb, eng=eng):
                    src_ = pT[pb][:, :].rearrange("p (c t) -> p c t", c=8)[:, :, 0:rows]
                    dst = uT[:, half * 8:half * 8 + 8, col0:col0 + rows]
                    if eng == 'act':
                        return e.copy(out=dst, in_=src_)
                    return e.tensor_copy(out=dst, in_=src_)
                P.add(eng, ev, reads=[('pT', pb)], writes=[('uT', ti, half)])

        n = len(tiles)
        stA(0)
        if n > 1:
            stA(1)
        stB(0)
        for ti in range(n):
            if ti + 2 < n:
                stA(ti + 2)
            if ti + 1 < n:
                stB(ti + 1)
            stC(ti)
        P.run()


def phase_OUT(nc, ctx, T):
    with ExitStack() as es:
        def sb(name, shape, dt):
            return es.enter_context(nc.sbuf_tensor(name, list(shape), dt))
        mixedT = sb("mixedTo", [128, 16, NOWN], BF16)
        wb = [sb("wbo%d" % i, [128, 16, 512], BF16) for i in range(2)]
        NS = 4
        xr = [sb("xro%d" % i, [128, 512], F32) for i in range(NS)]
        ho = [sb("hoo%d" % i, [128, 512], F32) for i in range(NS)]
        pm = [es.enter_context(nc.psum_tensor("pmo%d" % i, [128, 512], F32)) for i in range(4)]
        P = Phase(ctx, "OUT")
        w3 = T['w_out'].rearrange("(c p) n -> p c n", p=128)
        P.add('sp', dma(mixedT[:, 0:8, :], T['mixT'][:, 0:8, :]), writes=['mixedTa'], dsem='mixedTa')
        P.add('sp', dma(mixedT[:, 8:16, :], T['mixT'][:, 8:16, :]), writes=['mixedTm'], dsem='mixedTm')
        k = 0
        blocks = [(cb, tt) for cb in range(4) for tt in range(16)]

        def load_x(kk):
            cb_, tt_ = blocks[kk]
            s_ = kk % NS
            P.add('sp', dma(xr[s_][:, :], T['xs'][tt_ * 128:(tt_ + 1) * 128, cb_ * 512:(cb_ + 1) * 512]),
                  writes=[('xr', s_)], dsem=('xr', s_))
        load_x(0)
        load_x(1)
        for cb in range(4):
            ws = cb % 2
            P.add('pool', dma(wb[ws][:, 0:8, :], w3[:, 0:8, cb * 512:(cb + 1) * 512]), writes=[('wbA', ws)], dsem=('wbA', ws))
            P.add('pool', dma(wb[ws][:, 8:16, :], w3[:, 8:16, cb * 512:(cb + 1) * 512]), writes=[('wbB', ws)], dsem=('wbB', ws))
            for tt in range(16):
                pb = k % 4
                s = k % NS
                if k + 2 < len(blocks):
                    load_x(k + 2)
                k += 1

                def mm(e, ws=ws, tt=tt, pb=pb, c0=0):
                    ins = None
                    for c in range(c0, c0 + 8):
                        ins = e.matmul(pm[pb][:, :], lhsT=mixedT[:, c, tt * 128:(tt + 1) * 128], rhs=wb[ws][:, c, :],
                                       start=(c == 0), stop=(c == 15))
                    return ins
                P.add('pe', mm, reads=['mixedTa', ('wbA', ws)], writes=[('pm', pb)])
                P.add('pe', lambda e, mm=mm: mm(e, c0=8), reads=['mixedTm', ('wbB', ws)], writes=[('pm', pb)])
                P.add('dve', lambda e, s=s, pb=pb: e.tensor_tensor(out=ho[s][:, :], in0=pm[pb][:, :], in1=xr[s][:, :], op=ALU.add),
                      reads=[('pm', pb), ('xr', s)], writes=[('ho', s)])
                P.add('sp', dma(T['H1'][tt * 128:(tt + 1) * 128, cb * 512:(cb + 1) * 512], ho[s][:, :]),
                      reads=[('ho', s)], dsem=('hoo', s))
        P.run()


def phase_FFN(nc, ctx, T):
    HT = 1024
    NFC = DFF // 128
    with ExitStack() as es:
        def sb(name, shape, dt):
            return es.enter_context(nc.sbuf_tensor(name, list(shape), dt))
        u2T = sb("u2T", [128, 16, NOWN], BF16)
        GRP = 11
        wg0 = sb("wg0p", [128, 16, 256], BF16)
        wu0 = sb("wu0p", [128, 16, 256], BF16)
        wd0 = sb("wd0p", [128, GRP, 512], BF16)
        g3 = T['w_gate'].rearrange("(c p) n -> p c n", p=128)
        u3 = T['w_up'].rearrange("(c p) n -> p c n", p=128)
        d3 = T['w_down'].rearrange("(c p) n -> p c n", p=128)

        def pf_gu(P):
            P.add('pool', dma(wg0[:, :, :], g3[:, :, 0:256]), writes=[('wg', 0)], dsem=('wg', 0))
            P.add('pool', dma(wu0[:, :, :], u3[:, :, 0:256]), writes=[('wu', 0)], dsem=('wu', 0))

        def pf_d(P):
            P.add('pool', dma(wd0[:, :, :], d3[:, 0:GRP, 0:512]), writes=[('wd', 0)], dsem=('wd', 0))
        tiles = [(T['H1'][i * 128:(i + 1) * 128, :], 128, i * 128) for i in range(16)]
        norm_transpose(nc, ctx, "F1", tiles, T['norm_ffn'][0:1, :], T['consts'], u2T, pre_ops=pf_gu)
        hT = sb("hTf", [128, NFC, HT], BF16)
        for half in range(2):
            tok0 = half * HT
            with ExitStack() as es2:
                def sb2(name, shape, dt):
                    return es2.enter_context(nc.sbuf_tensor(name + "_h%d" % half, list(shape), dt))
                wg = [wg0, sb2("wg1", [128, 16, 256], BF16)]
                wu = [wu0, sb2("wu1", [128, 16, 256], BF16)]
                sg = [sb2("sg%d" % i, [128, 512], F32) for i in range(2)]
                pg = [es2.enter_context(nc.psum_tensor("pg%d_h%d" % (i, half), [128, 512], F32)) for i in range(2)]
                pu = [es2.enter_context(nc.psum_tensor("pu%d_h%d" % (i, half), [128, 512], F32)) for i in range(2)]
                P = Phase(ctx, "F2_%d" % half)
                k = 0
                for blk in range(NFC // 2):
                    ws = blk % 2
                    if blk > 0:
                        P.add('pool', dma(wg[ws][:, :, :], g3[:, :, blk * 256:(blk + 1) * 256]), writes=[('wg', ws)], dsem=('wg', ws))
                        P.add('pool', dma(wu[ws][:, :, :], u3[:, :, blk * 256:(blk + 1) * 256]), writes=[('wu', ws)], dsem=('wu', ws))
                    if blk == NFC // 2 - 2:
                        pf_d(P)
                    for fl in range(2):
                        fc = blk * 2 + fl
                        for tb in range(HT // 512):
                            pb = k % 2
                            k += 1

                            def mm(e, ws=ws, fl=fl, tb=tb, pb=pb):
                                ins = None
                                for c in range(16):
                                    ins = e.matmul(pg[pb][:, :], lhsT=wg[ws][:, c, fl * 128:(fl + 1) * 128],
                                                   rhs=u2T[:, c, tok0 + tb * 512:tok0 + (tb + 1) * 512], start=(c == 0), stop=(c == 15))
                                for c in range(16):
                                    ins = e.matmul(pu[pb][:, :], lhsT=wu[ws][:, c, fl * 128:(fl + 1) * 128],
                                                   rhs=u2T[:, c, tok0 + tb * 512:tok0 + (tb + 1) * 512], start=(c == 0), stop=(c == 15))
                                return ins
                            P.add('pe', mm, reads=[('wg', ws), ('wu', ws)], writes=[('pg', pb), ('pu', pb)])
                            P.add('act', lambda e, pb=pb: e.activation(out=sg[pb][:, :], in_=pg[pb][:, :], func=AF.Silu),
                                  reads=[('pg', pb)], writes=[('sg', pb)])
                            P.add('dve', lambda e, pb=pb, fc=fc, tb=tb: e.tensor_tensor(
                                out=hT[:, fc, tb * 512:(tb + 1) * 512], in0=pu[pb][:, :], in1=sg[pb][:, :], op=ALU.mult),
                                reads=[('pu', pb), ('sg', pb)], writes=[('hT', fc, tb)])
                P.run()
            with ExitStack() as es3:
                def sb3(name, shape, dt):
                    return es3.enter_context(nc.sbuf_tensor(name + "_h%d" % half, list(shape), dt))
                wd = [wd0, sb3("wd1", [128, GRP, 512], BF16)]
                NS = 3
                NL = 4
                h1s = [sb3("h1s%d" % i, [128, 512], F32) for i in range(NL)]
                h2s = [sb3("h2s%d" % i, [128, 512], F32) for i in range(NS)]
                pd = [es3.enter_context(nc.psum_tensor("pd%d_h%d" % (i, half), [128, 512], F32)) for i in range(8)]
                P = Phase(ctx, "F3_%d" % half)
                k = 0
                wcnt = 0
                evs = [(cb_, tt_) for cb_ in range(4) for tt_ in range(8)]

                def load_h1(kk):
                    cb_, tt_ = evs[kk]
                    r0_ = tok0 + tt_ * 128
                    P.add('sp', dma(h1s[kk % NL][:, :], T['H1'][r0_:r0_ + 128, cb_ * 512:(cb_ + 1) * 512]),
                          writes=[('h1s', kk % NL)], dsem=('h1s', kk % NL))
                load_h1(0)
                load_h1(1)
                for cb in range(4):
                    for fg in range(NFC // GRP):
                        ws = wcnt % 2
                        wcnt += 1
                        if wcnt > 1:
                            P.add('pool', dma(wd[ws][:, :, :], d3[:, fg * GRP:(fg + 1) * GRP, cb * 512:(cb + 1) * 512]),
                                  writes=[('wd', ws)], dsem=('wd', ws))
                        for tt in range(8):
                            def mm(e, ws=ws, fg=fg, tt=tt):
                                ins = None
                                for j in range(GRP):
                                    fc = fg * GRP + j
                                    ins = e.matmul(pd[tt][:, :], lhsT=hT[:, fc, tt * 128:(tt + 1) * 128], rhs=wd[ws][:, j, :],
                                                   start=(fc == 0), stop=(fc == NFC - 1))
                                return ins
                            P.add('pe', mm, reads=[('wd', ws)], writes=[('pd', tt)])
                    if half == 0 and cb == 3:
                        pf_gu(P)
                    for tt in range(8):
                        s = k % NS
                        sl = k % NL
                        if k + 2 < len(evs):
                            load_h1(k + 2)
                        k += 1
                        r0 = tok0 + tt * 128
                        P.add('dve', lambda e, s=s, sl=sl, tt=tt: e.tensor_tensor(out=h2s[s][:, :], in0=pd[tt][:, :], in1=h1s[sl][:, :], op=ALU.add),
                              reads=[('pd', tt), ('h1s', sl)], writes=[('h2s', s)])
                        P.add('sp', dma(T['H2'][r0:r0 + 128, cb * 512:(cb + 1) * 512], h2s[s][:, :]),
                              reads=[('h2s', s)], dsem=('h2so', s))
                P.run()


def phase_FIN(nc, ctx, T):
    with ExitStack() as es:
        def sb(name, shape, dt):
            return es.enter_context(nc.sbuf_tensor(name, list(shape), dt))
        NB = 3
        xb = [sb("xbf%d" % i, [128, D], F32) for i in range(NB)]
        ob = [sb("obf%d" % i, [128, D], F32) for i in range(NB)]
        junk = sb("junkf", [128, D], BF16)
        gbc = sb("gbcf", [128, D], F32)
        st = [sb("stf%d" % i, [128, 4], F32) for i in range(NB)]
        P = Phase(ctx, "FIN")
        P.add('sp', dma(gbc[:, :], T['norm_final'][0:1, :].partition_broadcast(128)), writes=['gbc'], dsem='gbc')

        def stA(tt):
            s = tt % NB
            P.add('sp', dma(xb[s][:, :], T['H2'][tt * 128:(tt + 1) * 128, :]), writes=[('xb', s)], dsem=('xb', s))

        def stB(tt):
            s = tt % NB
            P.add('act', lambda e: e.activation(out=junk[:, :], in_=xb[s][:, :], func=AF.Square, accum_out=st[s][:, 0:1]),
                  reads=[('xb', s)], writes=['junk', ('st0', s)])
            P.add('act', lambda e: e.activation(out=st[s][:, 1:2], in_=st[s][:, 0:1], func=AF.Sqrt, scale=1.0 / D, bias=EPS),
                  reads=[('st0', s)], writes=[('st1', s)])
            P.add('dve', lambda e: e.reciprocal(out=st[s][:, 2:3], in_=st[s][:, 1:2]), reads=[('st1', s)], writes=[('st2', s)])
            P.add('dve', lambda e: e.scalar_tensor_tensor(out=ob[s][:, :], in0=xb[s][:, :], scalar=st[s][:, 2:3], in1=gbc[:, :],
                                                          op0=ALU.mult, op1=ALU.mult),
                  reads=[('xb', s), ('st2', s), 'gbc'], writes=[('ob', s)])

        def stC(tt):
            s = tt % NB
            P.add('act', dma(T['out'][tt * 128:(tt + 1) * 128, :], ob[s][:, :]), reads=[('ob', s)], dsem=('obo', s))

        stA(0)
        stA(1)
        for tt in range(16):
            if tt + 2 < 16:
                stA(tt + 2)
            stB(tt)
            if tt >= 1:
                stC(tt - 1)
        stC(15)
        P.run()


def f32c(a):
    return np.ascontiguousarray(np.asarray(a, np.float32))


def make_in_maps(inp):
    x = np.asarray(inp['x'], np.float32)
    meta = np.asarray(inp['meta_tokens'], np.float32)
    consts = _consts()
    w_out = f32c(inp['w_out'][0])
    w_gate = f32c(inp['w_gate'][0])
    w_up = f32c(inp['w_up'][0])
    w_down = f32c(inp['w_down'][0])
    maps = []
    for core in range(8):
        b, hf = core // 2, core % 2
        idx = _local_index(hf)
        xs = np.zeros((NT, D), np.float32)
        real = idx >= 16
        xs[real] = x[b, idx[real] - 16]
        m = (idx >= 0) & (idx < 16)
        xs[m] = meta[idx[m]]
        cosT, sinT = _rope_tables(idx)
        kbias = np.where(idx[4096:4128] >= 0, 0.0, NEGB).astype(np.float32).reshape(32, 1)
        w_in = np.array(inp['w_in'][0], np.float32)
        if hf == 1:
            g = w_in[:, C_MG:C_MG + 16].copy()
            w_in[:, C_MG:C_MG + 8] = g[:, 8:16]
            w_in[:, C_MG + 8:C_MG + 16] = g[:, 0:8]
        gbias = f32c(inp['ml_gate_bias'])[0].reshape(1, 16).copy()
        cwt = f32c(inp['ml_conv_w'])[0]
        if hf == 1:
            gbias = np.concatenate([gbias[:, 8:16], gbias[:, 0:8]], 1)
            cwt = cwt[::-1]
        mbias = np.full((32, 2), NEGB, np.float32)
        mbias[0:16, 0] = np.where(idx[4096:4112] >= 0, 0.0, NEGB)
        mbias[16:32, 1] = np.where(idx[4112:4128] >= 0, 0.0, NEGB)
        convw_T = np.ascontiguousarray(cwt.reshape(5, 8, 128).transpose(2, 1, 0)).reshape(128, 40)
        convb_T = np.ascontiguousarray(f32c(inp['ml_conv_b'])[0].reshape(8, 128).T)
        maps.append({
            'xs': xs, 'w_in': np.ascontiguousarray(w_in), 'norm_mix': np.asarray(inp['norm_mix'], np.float32).reshape(1, D),
            'consts': consts, 'cosT': cosT, 'sinT': sinT,
            'lq1': f32c(inp['da_lambda_q1']).reshape(1, 64), 'lk1': f32c(inp['da_lambda_k1']).reshape(1, 64),
            'lq2': f32c(inp['da_lambda_q2']).reshape(1, 64), 'lk2': f32c(inp['da_lambda_k2']).reshape(1, 64),
            'dahn_T': np.ascontiguousarray(f32c(inp['da_head_norm'])[0].T),
            'kbias': kbias,
            'gbias': gbias, 'mbias': mbias, 'convw_T': convw_T, 'convb_T': convb_T,
            'mlhn': f32c(inp['ml_head_norm']).reshape(1, 1024),
            'w_out': w_out, 'w_gate': w_gate, 'w_up': w_up, 'w_down': w_down,
            'norm_ffn': f32c(inp['norm_ffn']).reshape(1, D), 'norm_final': f32c(inp['norm_final']).reshape(1, D),
        })
    return maps


def kernel(**inp):
    nc = build(DEBUG_STOP)
    maps = make_in_maps(inp)
    res = run_bass_kernel_spmd(nc, maps, core_ids=list(range(8)))
    if DEBUG_STOP is not None:
        return res
    outf = np.zeros((4, 4096, D), np.float32)
    for core in range(8):
        b, hf = core // 2, core % 2
        o = res.results[core]['out']
        if hf == 0:
            outf[b, 0:2048] = o
        else:
            outf[b, 2048:4096] = o[::-1]
    return outf
```
